# Optimizing a Trainium2 kernel written in Bass

```python
import math
import jax, jax.numpy as jnp
from jax import lax
import numpy as np

D_MODEL = 2048
BATCH = 8
SEQ = 2048
DEPTH = 2

N_MEM = 256
HEAD_DIM = 128
MIX_HEADS = 12
MIX_WIDTH = MIX_HEADS * HEAD_DIM
MEM_HEADS = 4
MEM_WIDTH = MEM_HEADS * HEAD_DIM
DIFF_QK_DIM = HEAD_DIM // 2
ROPE_THETA = 10000.0
Q_BLOCK = 128
MLSTM_CHUNK = 128
PEER_HEADS = 8
PEER_NKEYS = 128
PEER_NEXPERTS = PEER_NKEYS * PEER_NKEYS
PEER_QDIM = 256
PEER_SUBK = PEER_QDIM // 2
PEER_TOPK = 16
PEER_TOKEN_BLOCK = 128
N_MIXERS = 2
N_ATTN_LAYERS = (DEPTH + 1) // 2
N_MLSTM_LAYERS = DEPTH // 2
DN_ALPHA = (2.0 * DEPTH) ** 0.25
DN_BETA = (8.0 * DEPTH) ** -0.25
LN_EPS = 1e-5
ATTN_IN = 3 * MIX_WIDTH + MEM_WIDTH
MLSTM_IN = 4 * MIX_WIDTH + MEM_WIDTH + 4 * MIX_HEADS

kernel_name = "hybrid_diffattn_mlstm_peer_encoder"

F32 = jnp.float32


def layer_norm(x, g, b):
    xf = x.astype(F32)
    mu = xf.mean(-1, keepdims=True)
    var = jnp.square(xf - mu).mean(-1, keepdims=True)
    return ((xf - mu) * lax.rsqrt(var + LN_EPS)).astype(x.dtype) * g + b


def head_rms_norm(h, g):
    hf = h.astype(F32)
    return hf * lax.rsqrt(jnp.mean(hf * hf, -1, keepdims=True) + LN_EPS) * g.astype(F32)


def rope(t, positions):
    d = t.shape[-1]
    half = d // 2
    inv = jnp.power(ROPE_THETA, -jnp.arange(half, dtype=F32) * (2.0 / d))
    ang = positions.astype(F32)[..., None] * inv
    ang = ang.reshape(ang.shape[:2] + (1,) * (t.ndim - 3) + (half,))
    cos, sin = jnp.cos(ang), jnp.sin(ang)
    tf = t.astype(F32)
    t1, t2 = tf[..., :half], tf[..., half:]
    return jnp.concatenate([t1 * cos - t2 * sin, t2 * cos + t1 * sin], -1).astype(t.dtype)


def blocked_diff_attention(q, k, v, lam):
    B, S, H, _, dq = q.shape
    nb = S // Q_BLOCK
    scale = dq ** -0.5
    qb = jnp.moveaxis(q.reshape(B, nb, Q_BLOCK, H, 2, dq), 1, 0)

    def block(qblk):
        s = jnp.einsum('bqhmd,bkhmd->bhmqk', qblk, k).astype(F32) * scale
        p = jax.nn.softmax(s, axis=-1)
        a = p[:, :, 0] - lam * p[:, :, 1]
        return jnp.einsum('bhqk,bkhd->bqhd', a.astype(v.dtype), v)

    out = lax.map(block, qb)
    return jnp.moveaxis(out, 0, 1).reshape(B, S, H, v.shape[-1])


def diff_attention_mixer(q, k, v, positions, lam_params, norm_g, lam_init):
    B, S, _ = q.shape
    qh = rope(q.reshape(B, S, MIX_HEADS, 2, DIFF_QK_DIM), positions)
    kh = rope(k.reshape(B, S, MIX_HEADS, 2, DIFF_QK_DIM), positions)
    vh = v.reshape(B, S, MIX_HEADS, HEAD_DIM)
    lp = lam_params.astype(F32)
    lam = jnp.exp(jnp.sum(lp[0] * lp[1])) - jnp.exp(jnp.sum(lp[2] * lp[3])) + lam_init
    out = blocked_diff_attention(qh, kh, vh, lam)
    out = head_rms_norm(out, norm_g) * (1.0 - lam_init)
    return out.reshape(B, S, MIX_WIDTH).astype(q.dtype)


def mlstm_scan(q, k, v, ig, lf):
    B, H, S, d = q.shape
    L = MLSTM_CHUNK
    nc = S // L
    qc = q.reshape(B, H, nc, L, d)
    kc = k.reshape(B, H, nc, L, d)
    vc = v.reshape(B, H, nc, L, d)
    igc = ig.reshape(B, H, nc, L)
    b = jnp.cumsum(lf.reshape(B, H, nc, L), axis=-1)
    b_last = b[..., -1]
    a = b_last[..., None] - b + igc
    a_max = a.max(-1)
    wa = jnp.exp(a - a_max[..., None])
    kw = kc * wa[..., None]
    kv_chunk = jnp.einsum('bhcsk,bhcsv->bhckv', kw, vc)
    k_chunk = kw.sum(3)

    def step(carry, inp):
        C, n, m = carry
        bl, am, kvc, kcs = inp
        m_new = jnp.maximum(bl + m, am)
        decay = jnp.exp(bl + m - m_new)
        w = jnp.exp(am - m_new)
        C_new = decay[..., None, None] * C + w[..., None, None] * kvc
        n_new = decay[..., None] * n + w[..., None] * kcs
        return (C_new, n_new, m_new), (C, n, m)

    init = (jnp.zeros((B, H, d, d), F32), jnp.zeros((B, H, d), F32), jnp.zeros((B, H), F32))
    xs = (jnp.moveaxis(b_last, 2, 0), jnp.moveaxis(a_max, 2, 0),
          jnp.moveaxis(kv_chunk, 2, 0), jnp.moveaxis(k_chunk, 2, 0))
    _, (C_s, n_s, m_s) = lax.scan(step, init, xs)
    C_s = jnp.moveaxis(C_s, 0, 2)
    n_s = jnp.moveaxis(n_s, 0, 2)
    m_s = jnp.moveaxis(m_s, 0, 2)

    mask = jnp.tril(jnp.ones((L, L), dtype=bool))
    Dm = b[..., :, None] - b[..., None, :] + igc[..., None, :]
    Dm = jnp.where(mask, Dm, -jnp.inf)
    inter = b + m_s[..., None]
    m_j = jnp.maximum(inter, Dm.max(-1))
    Wd = jnp.exp(Dm - m_j[..., None])
    wi = jnp.exp(inter - m_j)
    qk = jnp.einsum('bhcjd,bhcsd->bhcjs', qc, kc) * Wd
    num = wi[..., None] * jnp.einsum('bhcjk,bhckv->bhcjv', qc, C_s) + jnp.einsum('bhcjs,bhcsv->bhcjv', qk, vc)
    den = wi * jnp.einsum('bhcjk,bhck->bhcj', qc, n_s) + qk.sum(-1)
    h = num / jnp.maximum(jnp.abs(den), jnp.exp(-m_j))[..., None]
    return h.reshape(B, H, S, d)


def mlstm_mixer(q, k, v, o, gates, gate_bias, norm_g):
    B, S, _ = q.shape

    def heads(t):
        return t.reshape(B, S, MIX_HEADS, HEAD_DIM).transpose(0, 2, 1, 3).astype(F32)

    qh, kh, vh = heads(q), heads(k) * (HEAD_DIM ** -0.5), heads(v)
    g = (gates.astype(F32).reshape(B, S, 4, MIX_HEADS) + gate_bias.astype(F32)).transpose(0, 2, 3, 1)
    ig_f, lf_f = g[:, 0], jax.nn.log_sigmoid(g[:, 1])
    ig_b, lf_b = g[:, 2], jax.nn.log_sigmoid(g[:, 3])
    h_f = mlstm_scan(qh, kh, vh, ig_f, lf_f)
    rev = lambda t: jnp.flip(t, axis=2)
    h_b = rev(mlstm_scan(rev(qh), rev(kh), rev(vh), rev(ig_b), rev(lf_b)))
    hs = head_rms_norm((h_f + h_b).transpose(0, 2, 1, 3), norm_g)
    out = jax.nn.sigmoid(o.astype(F32)).reshape(B, S, MIX_HEADS, HEAD_DIM) * hs
    return out.reshape(B, S, MIX_WIDTH).astype(q.dtype)


def memory_attention(qm, mem_k, mem_v):
    s = jnp.einsum('bshd,bmhd->bhsm', qm, mem_k).astype(F32) * (HEAD_DIM ** -0.5)
    p = jax.nn.softmax(s, axis=-1)
    return jnp.einsum('bhsm,bmhd->bshd', p.astype(mem_v.dtype), mem_v)


def peer_ffn(x, w_query, sub_keys, u_tab, v_tab):
    B, S, D = x.shape
    T = B * S
    xt = x.reshape(T, D)
    qry = (xt @ w_query).reshape(T, PEER_HEADS, 2, PEER_SUBK)
    s = jnp.einsum('thps,hpns->thpn', qry, sub_keys).astype(F32)
    sv, si = lax.top_k(s, PEER_TOPK)
    cand = (sv[:, :, 0, :, None] + sv[:, :, 1, None, :]).reshape(T, PEER_HEADS, PEER_TOPK * PEER_TOPK)
    cidx = (si[:, :, 0, :, None] * PEER_NKEYS + si[:, :, 1, None, :]).reshape(T, PEER_HEADS, PEER_TOPK * PEER_TOPK)
    top_s, pos = lax.top_k(cand, PEER_TOPK)
    eidx = jnp.take_along_axis(cidx, pos, axis=-1)
    gate = jax.nn.softmax(top_s, axis=-1)
    nb = T // PEER_TOKEN_BLOCK

    def block(args):
        xb, eb, gb = args
        act = jax.nn.gelu(jnp.einsum('thkd,td->thk', u_tab[eb], xb).astype(F32), approximate=False)
        w = (gb * act).astype(xb.dtype)
        return jnp.einsum('thk,thkd->td', w, v_tab[eb])

    out = lax.map(block, (xt.reshape(nb, PEER_TOKEN_BLOCK, D),
                          eidx.reshape(nb, PEER_TOKEN_BLOCK, PEER_HEADS, PEER_TOPK),
                          gate.reshape(nb, PEER_TOKEN_BLOCK, PEER_HEADS, PEER_TOPK)))
    return out.reshape(B, S, D)


def setup_inputs(seed: int = 0) -> dict:
    key = jax.random.key(seed)
    ks = jax.random.split(key, 24)
    D = D_MODEL
    nrm = lambda k, shape, sc: jax.random.normal(k, shape, F32) * sc
    x = nrm(ks[0], (BATCH, SEQ, D), 1.0)
    mem = nrm(ks[1], (BATCH, N_MEM, D), 1.0)
    positions = jnp.broadcast_to(jnp.arange(SEQ, dtype=jnp.int32)[None, :], (BATCH, SEQ))
    attn_w_in = nrm(ks[2], (N_ATTN_LAYERS, D, ATTN_IN), D ** -0.5)
    attn_lambda = nrm(ks[3], (N_ATTN_LAYERS, 4, DIFF_QK_DIM), 0.1)
    attn_head_norm = 1.0 + nrm(ks[4], (N_ATTN_LAYERS, MIX_HEADS, HEAD_DIM), 0.02)
    mlstm_w_in = nrm(ks[5], (N_MLSTM_LAYERS, D, MLSTM_IN), D ** -0.5)
    noise = nrm(ks[6], (N_MLSTM_LAYERS, 4, MIX_HEADS), 0.1)
    f_base = jnp.linspace(3.0, 6.0, MIX_HEADS, dtype=F32)
    base = jnp.stack([jnp.zeros_like(f_base), f_base, jnp.zeros_like(f_base), f_base])[None]
    mlstm_gate_bias = base + noise
    mlstm_head_norm = 1.0 + nrm(ks[7], (N_MLSTM_LAYERS, MIX_HEADS, HEAD_DIM), 0.02)
    mem_w_kv = nrm(ks[8], (D, 2 * MEM_WIDTH), D ** -0.5)
    w_out = nrm(ks[9], (DEPTH, MIX_WIDTH + MEM_WIDTH, D), DN_BETA * (MIX_WIDTH + MEM_WIDTH) ** -0.5)
    ln_mix_g = 1.0 + nrm(ks[10], (DEPTH, D), 0.02)
    ln_mix_b = nrm(ks[11], (DEPTH, D), 0.02)
    peer_w_query = nrm(ks[12], (DEPTH, D, PEER_HEADS * PEER_QDIM), D ** -0.5)
    peer_sub_keys = nrm(ks[13], (DEPTH, PEER_HEADS, 2, PEER_NKEYS, PEER_SUBK), PEER_SUBK ** -0.5)
    peer_u = nrm(ks[14], (DEPTH, PEER_NEXPERTS, D), D ** -0.5)
    peer_v = nrm(ks[15], (DEPTH, PEER_NEXPERTS, D), DN_BETA * PEER_HEADS ** -0.5)
    ln_ffn_g = 1.0 + nrm(ks[16], (DEPTH, D), 0.02)
    ln_ffn_b = nrm(ks[17], (DEPTH, D), 0.02)
    return {"x": x, "mem": mem, "positions": positions,
            "attn_w_in": attn_w_in, "attn_lambda": attn_lambda, "attn_head_norm": attn_head_norm,
            "mlstm_w_in": mlstm_w_in, "mlstm_gate_bias": mlstm_gate_bias, "mlstm_head_norm": mlstm_head_norm,
            "mem_w_kv": mem_w_kv, "w_out": w_out, "ln_mix_g": ln_mix_g, "ln_mix_b": ln_mix_b,
            "peer_w_query": peer_w_query, "peer_sub_keys": peer_sub_keys, "peer_u": peer_u, "peer_v": peer_v,
            "ln_ffn_g": ln_ffn_g, "ln_ffn_b": ln_ffn_b}


def reference(x, mem, positions, attn_w_in, attn_lambda, attn_head_norm, mlstm_w_in, mlstm_gate_bias,
              mlstm_head_norm, mem_w_kv, w_out, ln_mix_g, ln_mix_b, peer_w_query, peer_sub_keys, peer_u, peer_v,
              ln_ffn_g, ln_ffn_b):
    B, S, D = x.shape
    M = mem.shape[1]
    mem_kv = (mem @ mem_w_kv).reshape(B, M, 2, MEM_HEADS, HEAD_DIM)
    mem_k, mem_v = mem_kv[:, :, 0], mem_kv[:, :, 1]
    h = x
    for i in range(DEPTH):
        j = i // N_MIXERS
        if i % N_MIXERS == 0:
            proj = h @ attn_w_in[j]
            q, k, v, qm = jnp.split(proj, [MIX_WIDTH, 2 * MIX_WIDTH, 3 * MIX_WIDTH], axis=-1)
            lam_init = 0.8 - 0.6 * math.exp(-0.3 * i)
            mix = diff_attention_mixer(q, k, v, positions, attn_lambda[j], attn_head_norm[j], lam_init)
        else:
            proj = h @ mlstm_w_in[j]
            q, k, v, o, qm, gates = jnp.split(
                proj, [MIX_WIDTH, 2 * MIX_WIDTH, 3 * MIX_WIDTH, 4 * MIX_WIDTH, 4 * MIX_WIDTH + MEM_WIDTH], axis=-1)
            mix = mlstm_mixer(q, k, v, o, gates, mlstm_gate_bias[j], mlstm_head_norm[j])
        mem_out = memory_attention(qm.reshape(B, S, MEM_HEADS, HEAD_DIM), mem_k, mem_v).reshape(B, S, MEM_WIDTH)
        y = jnp.concatenate([mix, mem_out], axis=-1) @ w_out[i]
        h = layer_norm(DN_ALPHA * h + y, ln_mix_g[i], ln_mix_b[i])
        f = peer_ffn(h, peer_w_query[i], peer_sub_keys[i], peer_u[i], peer_v[i])
        h = layer_norm(DN_ALPHA * h + f, ln_ffn_g[i], ln_ffn_b[i])
    return h
```

```python
import math
from contextlib import ExitStack
import numpy as np
import concourse.bass as bass
import concourse.mybir as mybir
from concourse.bass_utils import run_bass_kernel_spmd

F32 = mybir.dt.float32
BF16 = mybir.dt.bfloat16
I32 = mybir.dt.int32
U32 = mybir.dt.uint32
AF = mybir.ActivationFunctionType
ALU = mybir.AluOpType
AX = mybir.AxisListType

D = 2048
SEQ = 2048
NT = SEQ // 128
NMEM = 256
DEPTH = 2
ALPHA = (2.0 * DEPTH) ** 0.25
EPS = 1e-5
NDMA_SLOTS = 12
NEG = -1.0e30
GRP = 8


class Buf:
    __slots__ = ("w", "r")

    def __init__(self):
        self.w = None
        self.r = {}


def bufs(n):
    return [Buf() for _ in range(n)]


class Sched:
    COMPUTE = ("pe", "act", "dve", "pool")

    def __init__(self, nc, es):
        self.nc = nc
        self.h = {"pe": nc.tensor, "act": nc.scalar, "dve": nc.vector, "pool": nc.gpsimd, "sp": nc.sync}
        self.sem = {}
        self.cnt = {}
        self.seen = {e: {} for e in self.h}
        for e in self.COMPUTE:
            self.sem[e] = es.enter_context(nc.semaphore("s_" + e))
            self.cnt[e] = 0
        self.dma_slots = {}
        for qn in ("sp", "pool", "act"):
            self.dma_slots[qn] = []
            for i in range(NDMA_SLOTS):
                nm = "d_%s%d" % (qn, i)
                self.sem[nm] = es.enter_context(nc.semaphore(nm))
                self.cnt[nm] = 0
                self.dma_slots[qn].append(nm)
        self.dma_rr = {qn: 0 for qn in self.dma_slots}
        self.nops = 0

    def _mult(self, e):
        return 1 if e in self.COMPUTE else 16

    def _need(self, eng, deps):
        for e, c in deps.items():
            if e == eng and eng == "pe":
                continue
            if self.seen[eng].get(e, 0) < c:
                self.seen[eng][e] = c
                self.h[eng].wait_ge(self.sem[e], c * self._mult(e))

    @staticmethod
    def _deps(reads, writes):
        deps = {}
        for b in reads:
            if b.w is not None:
                e, c = b.w
                if deps.get(e, 0) < c:
                    deps[e] = c
        for b in writes:
            if b.w is not None:
                e, c = b.w
                if deps.get(e, 0) < c:
                    deps[e] = c
            for e, c in b.r.items():
                if deps.get(e, 0) < c:
                    deps[e] = c
        return deps

    @staticmethod
    def _mark(who, c, reads, writes):
        for b in reads:
            if b.r.get(who, 0) < c:
                b.r[who] = c
        for b in writes:
            b.w = (who, c)
            b.r = {}

    def op(self, eng, fn, reads=(), writes=()):
        self._need(eng, self._deps(reads, writes))
        self.cnt[eng] += 1
        fn(self.h[eng]).then_inc(self.sem[eng], 1)
        self._mark(eng, self.cnt[eng], reads, writes)
        self.nops += 1

    def dma(self, qn, fn, reads=(), writes=()):
        slots = self.dma_slots[qn]
        slot = slots[self.dma_rr[qn] % len(slots)]
        self.dma_rr[qn] += 1
        deps = self._deps(reads, writes)
        if self.cnt[slot] > 0:
            deps[slot] = max(deps.get(slot, 0), self.cnt[slot])
        self._need(qn, deps)
        self.cnt[slot] += 1
        fn(self.h[qn]).then_inc(self.sem[slot], 16)
        self._mark(slot, self.cnt[slot], reads, writes)
        self.nops += 1

    def barrier(self):
        allc = {e: c for e, c in self.cnt.items() if c > 0}
        for eng in self.h:
            self._need(eng, dict(allc))

    def finish(self):
        for qn, slots in self.dma_slots.items():
            self._need(qn, {s: self.cnt[s] for s in slots if self.cnt[s] > 0})


class Ring:
    def __init__(self, tiles):
        self.t = tiles
        self.b = bufs(len(tiles))
        self.i = 0

    def next(self):
        k = self.i % len(self.t)
        self.i += 1
        return self.t[k], self.b[k]


class Builder:
    def __init__(self, nc, es, S, T):
        self.nc, self.es, self.S, self.T = nc, es, S, T
        self.ps = [es.enter_context(nc.psum_tensor("ps%d" % i, [128, 512], F32)) for i in range(8)]
        self.psb = bufs(8)
        sb = self.sbuf
        self.ident = sb(es, "ident", [128, 128], BF16)
        self.ident_b = Buf()
        S.dma("pool", lambda e: e.dma_start(out=self.ident[:], in_=T["c_ident"][:, :]), writes=[self.ident_b])
        self.memKT = sb(es, "memKT", [128, 4, 256], BF16)
        self.memKT_b = Buf()
        self.memV = sb(es, "memV", [128, 2, 4, 144], BF16)
        self.memV_b = Buf()
        self.eps_t = sb(es, "eps_t", [128, 1], F32)
        self.eps_b = Buf()
        S.op("pool", lambda e: e.memset(self.eps_t[:], EPS), writes=[self.eps_b])

    def sbuf(self, es, name, shape, dt):
        self.uid = getattr(self, "uid", 0) + 1
        return es.enter_context(self.nc.sbuf_tensor("%s_%d" % (name, self.uid), shape, dt))

    def load_T(self, pes, src, hT, hTb, src_bf16=False):
        S, nc = self.S, self.nc
        xin = Ring([self.sbuf(pes, "ldx%d" % i, [128, D], BF16 if src_bf16 else F32) for i in range(2)])
        xbr = None if src_bf16 else Ring([self.sbuf(pes, "ldb%d" % i, [128, D], BF16) for i in range(2)])
        for tt in range(NT):
            xt, xb_ = xin.next()
            S.dma("sp", lambda e: e.dma_start(out=xt[:], in_=src[tt * 128:(tt + 1) * 128, :]), writes=[xb_])
            if src_bf16:
                xb, xbb = xt, xb_
            else:
                xb, xbb = xbr.next()
                S.op("act", lambda e: e.activation(out=xb[:], in_=xt[:], func=AF.Copy), reads=[xb_], writes=[xbb])
            self.transpose_tile(xb, xbb, hT[:, :, tt * 128:(tt + 1) * 128], [hTb[tt]], tt)

    def transpose_tile(self, xb, xbb, out_ap, out_bufs, parity, banks=(6, 7)):
        S = self.S
        for half in range(2):
            k = banks[(2 * parity + half) % 2]
            pv = self.ps[k][:].bitcast(BF16)
            for c in range(8):
                dc = half * 8 + c
                S.op("pe", lambda e: e.transpose(out=pv[:, c * 128:(c + 1) * 128], in_=xb[:, dc * 128:(dc + 1) * 128],
                                                 identity=self.ident[:]),
                     reads=[xbb, self.ident_b], writes=[self.psb[k]])
            S.op("dve", lambda e: e.tensor_copy(out=out_ap[:, half * 8:(half + 1) * 8, :],
                                                in_=pv.rearrange("p (c t) -> p c t", c=8)),
                 reads=[self.psb[k]], writes=out_bufs)

    def load_w(self, wt, wb_, src_ap, q="pool"):
        self.S.dma(q, lambda e: e.dma_start(out=wt, in_=src_ap), writes=[wb_])

    def ln_store(self, z, zb, gt, bt, gbb, dst_ap, st, stb):
        S = self.S
        stats, mv, sd = st
        for c in range(4):
            S.op("dve", lambda e: e.bn_stats(out=stats[:, c, :], in_=z[:, c * 512:(c + 1) * 512]), reads=[zb], writes=[stb])
        S.op("dve", lambda e: e.bn_aggr(out=mv[:], in_=stats[:].rearrange("p a b -> p (a b)")), reads=[stb], writes=[stb])
        S.op("act", lambda e: e.activation(out=sd[:], in_=mv[:, 1:2], func=AF.Sqrt, bias=self.eps_t[:], scale=1.0),
             reads=[stb, self.eps_b], writes=[stb])
        S.op("dve", lambda e: e.reciprocal(out=sd[:], in_=sd[:]), reads=[stb], writes=[stb])
        S.op("dve", lambda e: e.tensor_scalar(out=z[:], in0=z[:], scalar1=mv[:, 0:1], scalar2=sd[:, 0:1],
                                              op0=ALU.subtract, op1=ALU.mult), reads=[zb, stb], writes=[zb])
        S.op("pool", lambda e: e.tensor_tensor(out=z[:], in0=z[:], in1=gt[:], op=ALU.mult), reads=[zb, gbb], writes=[zb])
        S.op("pool", lambda e: e.tensor_tensor(out=z[:], in0=z[:], in1=bt[:], op=ALU.add), reads=[zb, gbb], writes=[zb])
        S.dma("sp", lambda e: e.dma_start(out=dst_ap, in_=z[:]), reads=[zb])

    def load_ln(self, pes, g_ap, b_ap):
        gt = self.sbuf(pes, "ln_g", [128, D], F32)
        bt = self.sbuf(pes, "ln_b", [128, D], F32)
        gbb = Buf()
        self.S.dma("sp", lambda e: e.dma_start(out=gt[:], in_=g_ap.partition_broadcast(128)), writes=[gbb])
        self.S.dma("sp", lambda e: e.dma_start(out=bt[:], in_=b_ap.partition_broadcast(128)), writes=[gbb])
        stats = self.sbuf(pes, "ln_stats", [128, 4, 6], F32)
        mv = self.sbuf(pes, "ln_mv", [128, 2], F32)
        sd = self.sbuf(pes, "ln_sd", [128, 1], F32)
        return gt, bt, gbb, (stats, mv, sd), Buf()

    def phase_mem(self):
        S, T = self.S, self.T
        with ExitStack() as pes:
            memT = self.sbuf(pes, "memT", [128, 16, NMEM], BF16)
            memTb = bufs(2)
            wkv = self.sbuf(pes, "wkv", [128, 16, 1024], BF16)
            wkvb = Buf()
            src = T["mem_w_kv"].rearrange("(dc p) n -> p dc n", p=128)
            for i in range(2):
                self.load_w(wkv[:, :, i * 512:(i + 1) * 512], wkvb, src[:, :, i * 512:(i + 1) * 512])
            xr = [self.sbuf(pes, "mx%d" % i, [128, D], F32) for i in range(2)]
            xbr = [self.sbuf(pes, "mxb%d" % i, [128, D], BF16) for i in range(2)]
            for mc in range(2):
                xb_, xbb = Buf(), Buf()
                S.dma("sp", lambda e: e.dma_start(out=xr[mc][:], in_=T["mem"][mc * 128:(mc + 1) * 128, :]), writes=[xb_])
                S.op("act", lambda e: e.activation(out=xbr[mc][:], in_=xr[mc][:], func=AF.Copy), reads=[xb_], writes=[xbb])
                self.transpose_tile(xbr[mc], xbb, memT[:, :, mc * 128:(mc + 1) * 128], [memTb[mc]], mc)
            S.op("pool", lambda e: e.memset(self.memV[:], 1.0), writes=[self.memV_b])
            for hh in range(4):
                k = hh % 2
                for dc in range(16):
                    S.op("pe", lambda e: e.matmul(self.ps[k][:, 0:NMEM], lhsT=wkv[:, dc, hh * 128:(hh + 1) * 128],
                                                  rhs=memT[:, dc, :], start=dc == 0, stop=dc == 15),
                         reads=[wkvb] + memTb, writes=[self.psb[k]])
                S.op("act", lambda e: e.activation(out=self.memKT[:, hh, :], in_=self.ps[k][:, 0:NMEM], func=AF.Copy),
                     reads=[self.psb[k]], writes=[self.memKT_b])
            for mc in range(2):
                k = 2 + mc
                for dc in range(16):
                    S.op("pe", lambda e: e.matmul(self.ps[k][:, :], lhsT=memT[:, dc, mc * 128:(mc + 1) * 128],
                                                  rhs=wkv[:, dc, 512:1024], start=dc == 0, stop=dc == 15),
                         reads=[wkvb] + memTb, writes=[self.psb[k]])
                S.op("act", lambda e: e.activation(out=self.memV[:, mc, :, 0:128],
                                                   in_=self.ps[k][:].rearrange("p (h d) -> p h d", h=4), func=AF.Copy),
                     reads=[self.psb[k]], writes=[self.memV_b])
            S.barrier()

    def attn_core(self, QT, QTb, KT, KTb, krows, n_sc, v_ap, v_bufs, scale, consume, ring_p, mode=None):
        S = self.S
        k0, k1 = krows
        for jg in range(4):
            abank = [2, 3, 4, 5]
            if mode is None:
                scs = list(range(n_sc))
            elif mode["dir"] == "f":
                scs = list(range(0, 4 * jg + 4))
            else:
                scs = list(range(4 * jg, 16))
            pbufs = {}

            def emit_st(si, jg=jg, scs=scs):
                sc = scs[si]
                kst = (jg * 16 + si) % 2
                c0, c1 = 0, 512
                if mode is not None:
                    if mode["dir"] == "f" and sc >= 4 * jg:
                        c0 = (sc - 4 * jg) * 128
                    if mode["dir"] == "b" and sc <= 4 * jg + 3:
                        c1 = (sc - 4 * jg + 1) * 128
                S.op("pe", lambda e: e.matmul(self.ps[kst][:, c0:c1], lhsT=KT[k0:k1, sc * 128:(sc + 1) * 128],
                                              rhs=QT[k0:k1, jg * 512 + c0:jg * 512 + c1], start=True, stop=True),
                     reads=[QTb, KTb], writes=[self.psb[kst]])
                self.bg_tick()
                P, Pb = ring_p.next()
                if mode is None:
                    S.op("act", lambda e: e.activation(out=P[:, c0:c1], in_=self.ps[kst][:, c0:c1], func=AF.Exp, scale=scale),
                         reads=[self.psb[kst]], writes=[Pb])
                else:
                    Wt, Wtb = mode["wring"].next()
                    S.op("act", lambda e: e.activation(out=Wt[:, c0:c1], in_=mode["Rb"][:, jg * 512 + c0:jg * 512 + c1],
                                                       func=AF.Exp, bias=mode["bias"][:, sc, mode["col"]:mode["col"] + 1], scale=1.0),
                         reads=[mode["Rbb"], mode["biasb"]], writes=[Wtb])
                    S.op("dve", lambda e: e.tensor_tensor(out=P[:, c0:c1], in0=self.ps[kst][:, c0:c1], in1=Wt[:, c0:c1], op=ALU.mult),
                         reads=[self.psb[kst], Wtb], writes=[Pb])
                    if 4 * jg <= sc <= 4 * jg + 3:
                        d0 = (sc - 4 * jg) * 128
                        S.op("pool", lambda e: e.tensor_tensor(out=P[:, d0:d0 + 128], in0=P[:, d0:d0 + 128], in1=mode["tri"][:],
                                                               op=ALU.mult), reads=[Pb, mode["trib"]], writes=[Pb])
                pbufs[si] = (P, Pb)

            def emit_pv(si, jg=jg, scs=scs):
                sc = scs[si]
                P, Pb = pbufs.pop(si)
                for j4 in range(4):
                    jc = 4 * jg + j4
                    if mode is None:
                        first, last = 0, n_sc - 1
                    elif mode["dir"] == "f":
                        first, last = 0, jc
                    else:
                        first, last = jc, 15
                    if sc < first or sc > last:
                        continue
                    ab = abank[j4]
                    acc = self.ps[ab][:, 0:130]
                    S.op("pe", lambda e: e.matmul(acc, lhsT=P[:, j4 * 128:(j4 + 1) * 128], rhs=v_ap(sc),
                                                  start=sc == first, stop=sc == last),
                         reads=[Pb] + v_bufs, writes=[self.psb[ab]])

            emit_st(0)
            for si in range(len(scs)):
                if si + 1 < len(scs):
                    emit_st(si + 1)
                emit_pv(si)
            for j4 in range(4):
                ab = abank[j4]
                consume(4 * jg + j4, self.ps[ab][:, 0:130], self.psb[ab])

    def proj_fm(self, w, wb_, hT, hTb, jg, k):
        S = self.S
        for dc in range(16):
            S.op("pe", lambda e: e.matmul(self.ps[k][:, :], lhsT=w[:, dc, :], rhs=hT[:, dc, jg * 512:(jg + 1) * 512],
                                          start=dc == 0, stop=dc == 15),
                 reads=[wb_] + hTb[4 * jg:4 * jg + 4], writes=[self.psb[k]])

    def proj_tm(self, w, wb_, hT, hTb, evac):
        S = self.S
        for g in range(4):
            k = g % 2
            for i in range(4):
                tt = 4 * g + i
                for dc in range(16):
                    S.op("pe", lambda e: e.matmul(self.ps[k][:, i * 128:(i + 1) * 128], lhsT=hT[:, dc, tt * 128:(tt + 1) * 128],
                                                  rhs=w[:, dc, :], start=dc == 0, stop=dc == 15),
                         reads=[wb_, hTb[tt]], writes=[self.psb[k]])
            evac(g, self.ps[k], self.psb[k])

    def mem_heads(self, pes, w_src, col0, hT, hTb, mix_d, wts, ring_p, QT, QTb, mixh, mixhb):
        S = self.S
        sm = self.sbuf(pes, "mh_rs", [128, 1], F32)
        smb = Buf()
        for hh in range(4):
            w, wb_ = wts.next()
            self.load_w(w[:], wb_, w_src[:, :, col0 + hh * 128:col0 + (hh + 1) * 128])
            for jg in range(4):
                self.proj_fm(w, wb_, hT, hTb, jg, jg % 2)
                S.op("act", lambda e: e.activation(out=QT[:, jg * 512:(jg + 1) * 512], in_=self.ps[jg % 2][:, :], func=AF.Copy),
                     reads=[self.psb[jg % 2]], writes=[QTb])

            def consume(jc, acc, accb):
                S.op("dve", lambda e: e.reciprocal(out=sm[:], in_=acc[:, 128:129]), reads=[accb], writes=[smb])
                S.op("dve", lambda e: e.tensor_scalar(out=mixh[:, jc, :], in0=acc[:, 0:128], scalar1=sm[:, 0:1], scalar2=None,
                                                      op0=ALU.mult), reads=[accb, smb], writes=[mixhb])

            self.attn_core(QT, QTb, self.memKT[:, hh, :], self.memKT_b, (0, 128), 2,
                           lambda sc: self.memV[:, sc, hh, 0:130], [self.memV_b], 128 ** -0.5, consume, ring_p)
            c = 1536 + hh * 128
            S.dma("sp", lambda e: e.dma_start(out=mix_d.rearrange("(jc p) f -> p jc f", p=128)[:, :, c:c + 128], in_=mixh[:]),
                  reads=[mixhb])

    def phase_attn(self, h_d, mix_d, bg_tables=None):
        S, T = self.S, self.T
        lam_init = 0.8 - 0.6 * math.exp(-0.3 * 0)
        with ExitStack() as pes:
            hT = self.sbuf(pes, "hT", [128, 16, SEQ], BF16)
            hTb = bufs(NT)
            with ExitStack() as les:
                self.load_T(les, h_d, hT, hTb)
                S.barrier()
            if bg_tables:
                self.bg_start(pes, bg_tables, 12 * 2 * 64 + 4 * 8)
            sb = lambda n, s, d: self.sbuf(pes, n, s, d)
            CS = sb("CS", [128, SEQ], F32)
            SN = sb("SN", [128, SEQ], F32)
            rb = Buf()
            with ExitStack() as res:
                posi = self.sbuf(res, "posi", [128, SEQ], I32)
                y = self.sbuf(res, "ry", [128, SEQ], F32)
                kf = self.sbuf(res, "rkf", [128, SEQ], F32)
                ki = self.sbuf(res, "rki", [128, SEQ], I32)
                cst = self.sbuf(res, "rcst", [128, 2], F32)
                tb = Buf()
                S.dma("sp", lambda e: e.dma_start(out=posi[:], in_=T["pos"].partition_broadcast(128)), writes=[tb])
                S.dma("sp", lambda e: e.dma_start(out=cst[:], in_=T["c_rope"][:, :]), writes=[tb])
                S.op("dve", lambda e: e.tensor_copy(out=kf[:], in_=posi[:]), reads=[tb], writes=[tb])
                for which, dst in ((0, SN), (1, CS)):
                    S.op("dve", lambda e: e.tensor_scalar(out=y[:], in0=kf[:], scalar1=cst[:, 0:1], scalar2=0.25 * which,
                                                          op0=ALU.mult, op1=ALU.add), reads=[tb], writes=[rb])
                    S.op("dve", lambda e: e.tensor_copy(out=ki[:], in_=y[:]), reads=[rb], writes=[rb])
                    S.op("dve", lambda e: e.tensor_copy(out=dst[:], in_=ki[:]), reads=[rb], writes=[rb])
                    S.op("dve", lambda e: e.tensor_tensor(out=y[:], in0=y[:], in1=dst[:], op=ALU.subtract), reads=[rb], writes=[rb])
                    S.op("dve", lambda e: e.tensor_single_scalar(out=dst[:], in_=y[:], scalar=0.5, op=ALU.is_gt), reads=[rb], writes=[rb])
                    S.op("dve", lambda e: e.tensor_tensor(out=y[:], in0=y[:], in1=dst[:], op=ALU.subtract), reads=[rb], writes=[rb])
                    S.op("dve", lambda e: e.tensor_single_scalar(out=dst[:], in_=y[:], scalar=-0.5, op=ALU.is_lt), reads=[rb], writes=[rb])
                    S.op("dve", lambda e: e.tensor_tensor(out=y[:], in0=y[:], in1=dst[:], op=ALU.add), reads=[rb], writes=[rb])
                    S.op("act", lambda e: e.activation(out=dst[:], in_=y[:], func=AF.Sin, scale=2.0 * math.pi), reads=[rb], writes=[rb])
                S.op("dve", lambda e: e.tensor_scalar(out=SN[:], in0=SN[:], scalar1=cst[:, 1:2], scalar2=None, op0=ALU.mult),
                     reads=[rb, tb], writes=[rb])
                S.barrier()
            lamt = sb("lamt", [128, 4, 64], F32)
            lw = sb("lamw", [128, 8], F32)
            lb = Buf()
            S.dma("sp", lambda e: e.dma_start(out=lamt[:].rearrange("p a b -> p (a b)"),
                                              in_=T["attn_lambda"].partition_broadcast(128)), writes=[lb])
            lj = sb("lamj", [128, 64], F32)
            for i in range(2):
                S.op("dve", lambda e: e.scalar_tensor_tensor(out=lj[:], in0=lamt[:, 2 * i, :], scalar=1.0, in1=lamt[:, 2 * i + 1, :],
                                                             op0=ALU.mult, op1=ALU.mult, accum_out=lw[:, i:i + 1]),
                     reads=[lb], writes=[lb])
            S.op("act", lambda e: e.activation(out=lw[:, 2:4], in_=lw[:, 0:2], func=AF.Exp), reads=[lb], writes=[lb])
            S.op("dve", lambda e: e.tensor_tensor(out=lw[:, 4:5], in0=lw[:, 3:4], in1=lw[:, 2:3], op=ALU.subtract), reads=[lb], writes=[lb])
            S.op("dve", lambda e: e.tensor_single_scalar(out=lw[:, 5:6], in_=lw[:, 4:5], scalar=-lam_init, op=ALU.add), reads=[lb], writes=[lb])
            neglam = lw[:, 5:6]
            G = sb("Gn", [128, 1536], F32)
            Gb = Buf()
            S.dma("sp", lambda e: e.dma_start(out=G[:], in_=T["attn_head_norm"].partition_broadcast(128)), writes=[Gb])
            wts = Ring([sb("aw%d" % i, [128, 16, 128], BF16) for i in range(6)])
            QT = sb("QT", [128, SEQ], BF16)
            KT = sb("KT", [128, SEQ], BF16)
            QTb, KTb = Buf(), Buf()
            V = sb("Vaug", [128, 16, 144], BF16)
            Vb = Buf()
            S.op("pool", lambda e: e.memset(V[:], 1.0), writes=[Vb])
            ring_p = Ring([sb("P%d" % i, [128, 512], BF16) for i in range(3)])
            t1r = Ring([sb("rt1%d" % i, [128, 512], F32) for i in range(2)])
            t2r = Ring([sb("rt2%d" % i, [128, 512], F32) for i in range(2)])
            om = [sb("om%d" % i, [128, 16, 128], F32) for i in range(2)]
            omb = bufs(2)
            sm = sb("a_rs", [128, 1], F32)
            smb = Buf()
            sq = sb("a_sq", [128, 16, 128], F32)
            ms = sb("a_ms", [128, 16], F32)
            mixh = sb("mixh", [128, 16, 128], BF16)
            mixhb = Buf()
            W = T["attn_w_in"].rearrange("(dc p) n -> p dc n", p=128)
            WR = T["attn_w_rot"].rearrange("(dc p) n -> p dc n", p=128)
            for h in range(12):
                for qi, (dst, dstb) in enumerate(((QT, QTb), (KT, KTb))):
                    w, wb_ = wts.next()
                    wr, wrb = wts.next()
                    c = qi * 1536 + h * 128
                    self.load_w(w[:], wb_, W[:, :, c:c + 128])
                    self.load_w(wr[:], wrb, WR[:, :, c:c + 128])
                    for jg in range(4):
                        self.proj_fm(w, wb_, hT, hTb, jg, 0)
                        self.proj_fm(wr, wrb, hT, hTb, jg, 1)
                        t1, t1b = t1r.next()
                        t2, t2b = t2r.next()
                        js = slice(jg * 512, (jg + 1) * 512)
                        S.op("dve", lambda e: e.tensor_tensor(out=t1[:], in0=self.ps[0][:, :], in1=CS[:, js], op=ALU.mult),
                             reads=[self.psb[0], rb], writes=[t1b])
                        S.op("dve", lambda e: e.tensor_tensor(out=t2[:], in0=self.ps[1][:, :], in1=SN[:, js], op=ALU.mult),
                             reads=[self.psb[1], rb], writes=[t2b])
                        S.op("pool", lambda e: e.tensor_tensor(out=dst[:, js], in0=t1[:], in1=t2[:], op=ALU.add),
                             reads=[t1b, t2b], writes=[dstb])
                w, wb_ = wts.next()
                c = 2 * 1536 + h * 128
                self.load_w(w[:], wb_, W[:, :, c:c + 128])

                def evac_v(g, pst, pstb):
                    S.op("act", lambda e: e.activation(out=V[:, 4 * g:4 * g + 4, 0:128],
                                                       in_=pst[:].rearrange("p (i d) -> p i d", i=4), func=AF.Copy),
                         reads=[pstb], writes=[Vb])

                self.proj_tm(w, wb_, hT, hTb, evac_v)
                for m in range(2):
                    def consume(jc, acc, accb, m=m):
                        S.op("dve", lambda e: e.reciprocal(out=sm[:], in_=acc[:, 128:129]), reads=[accb], writes=[smb])
                        S.op("dve", lambda e: e.tensor_scalar(out=om[m][:, jc, :], in0=acc[:, 0:128], scalar1=sm[:, 0:1], scalar2=None,
                                                              op0=ALU.mult), reads=[accb, smb], writes=[omb[m]])

                    self.attn_core(QT, QTb, KT, KTb, (m * 64, (m + 1) * 64), 16, lambda sc: V[:, sc, 0:130], [Vb], 64 ** -0.5,
                                   consume, ring_p)
                o0 = om[0][:].rearrange("p a b -> p (a b)")
                o1 = om[1][:].rearrange("p a b -> p (a b)")
                S.op("dve", lambda e: e.scalar_tensor_tensor(out=o0, in0=o1, scalar=neglam, in1=o0, op0=ALU.mult, op1=ALU.add),
                     reads=[omb[0], omb[1], lb], writes=[omb[0]])
                self.head_norm_out(om[0], omb[0], sq, ms, G[:, h * 128:(h + 1) * 128], Gb, 1.0 - lam_init, None, mixh, mixhb, smb)
                S.dma("sp", lambda e: e.dma_start(out=mix_d.rearrange("(jc p) f -> p jc f", p=128)[:, :, h * 128:(h + 1) * 128],
                                                  in_=mixh[:]), reads=[mixhb])
            self.mem_heads(pes, W, 3 * 1536, hT, hTb, mix_d, wts, ring_p, QT, QTb, mixh, mixhb)
            self.bg_finish()
            S.barrier()

    def head_norm_out(self, a, ab, sq, ms, g_ap, gb, mult, og, mixh, mixhb, smb):
        S = self.S
        S.op("pool", lambda e: e.tensor_tensor(out=sq[:], in0=a[:], in1=a[:], op=ALU.mult), reads=[ab], writes=[smb])
        S.op("dve", lambda e: e.tensor_reduce(out=ms[:], in_=sq[:], axis=AX.X, op=ALU.add), reads=[smb], writes=[smb])
        S.op("act", lambda e: e.activation(out=ms[:], in_=ms[:], func=AF.Sqrt, bias=self.eps_t[:], scale=1.0 / 128.0),
             reads=[smb, self.eps_b], writes=[smb])
        S.op("dve", lambda e: e.reciprocal(out=ms[:], in_=ms[:]), reads=[smb], writes=[smb])
        S.op("dve", lambda e: e.tensor_tensor(out=a[:], in0=a[:], in1=ms[:].unsqueeze(2).to_broadcast([128, 16, 128]), op=ALU.mult),
             reads=[ab, smb], writes=[ab])
        if og is not None:
            S.op("pool", lambda e: e.tensor_tensor(out=a[:], in0=a[:], in1=og[0][:], op=ALU.mult), reads=[ab, og[1]], writes=[ab])
        S.op("dve", lambda e: e.scalar_tensor_tensor(out=mixh[:], in0=a[:], scalar=float(mult),
                                                     in1=g_ap.unsqueeze(1).to_broadcast([128, 16, 128]),
                                                     op0=ALU.mult, op1=ALU.mult), reads=[ab, gb], writes=[mixhb])

    def phase_mlstm(self, h_d, mix_d, bg_tables=None):
        S, T = self.S, self.T
        with ExitStack() as pes:
            hT = self.sbuf(pes, "hT", [128, 16, SEQ], BF16)
            hTb = bufs(NT)
            with ExitStack() as les:
                self.load_T(les, h_d, hT, hTb)
                S.barrier()
            if bg_tables:
                self.bg_start(pes, bg_tables, 12 * 80 + 4 * 8)
            sb = lambda n, s, d: self.sbuf(pes, n, s, d)
            W = T["mlstm_w_in"].rearrange("(dc p) n -> p dc n", p=128)
            XG = sb("XG", [48, SEQ], F32)
            CUM = sb("CUM", [48, SEQ], F32)
            SUF = sb("SUF", [48, SEQ], F32)
            gb_ = Buf()
            BIAS = sb("BIAS", [128, 16, 24], F32)
            biasb = Buf()
            with ExitStack() as ges:
                LF = self.sbuf(ges, "LF", [48, SEQ], F32)
                wg = self.sbuf(ges, "wg", [128, 16, 48], BF16)
                wgb = Buf()
                self.load_w(wg[:], wgb, W[:, :, 6656:6704])
                gbias = self.sbuf(ges, "gbias", [48, 1], F32)
                ones = self.sbuf(ges, "gones", [48, SEQ], F32)
                dsel = self.sbuf(ges, "dsel", [48, 3, 24], F32)
                cb = Buf()
                S.dma("sp", lambda e: e.dma_start(out=gbias[:], in_=T["gate_bias"][:, :]), writes=[cb])
                S.dma("sp", lambda e: e.dma_start(out=dsel[:], in_=T["c_dsel"][:, :, :]), writes=[cb])
                S.op("pool", lambda e: e.memset(ones[:], 1.0), writes=[cb])
                for jg in range(4):
                    js = slice(jg * 512, (jg + 1) * 512)
                    for dc in range(16):
                        S.op("pe", lambda e: e.matmul(self.ps[jg % 2][0:48, :], lhsT=wg[:, dc, :], rhs=hT[:, dc, js],
                                                      start=dc == 0, stop=dc == 15),
                             reads=[wgb] + hTb[4 * jg:4 * jg + 4], writes=[self.psb[jg % 2]])
                    S.op("dve", lambda e: e.tensor_scalar(out=XG[:, js], in0=self.ps[jg % 2][0:48, :], scalar1=gbias[:, 0:1],
                                                          scalar2=None, op0=ALU.add), reads=[self.psb[jg % 2], cb], writes=[gb_])
                S.op("act", lambda e: e.activation(out=LF[:], in_=XG[:], func=AF.Exp, scale=-1.0), reads=[gb_], writes=[gb_])
                S.op("act", lambda e: e.activation(out=LF[:], in_=LF[:], func=AF.Ln, bias=1.0, scale=1.0), reads=[gb_], writes=[gb_])
                S.op("dve", lambda e: e.tensor_single_scalar(out=LF[:], in_=LF[:], scalar=-1.0, op=ALU.mult), reads=[gb_], writes=[gb_])
                S.op("dve", lambda e: e.tensor_tensor_scan(out=CUM[:], data0=ones[:], data1=LF[:], initial=0.0,
                                                           op0=ALU.mult, op1=ALU.add), reads=[gb_, cb], writes=[gb_])
                S.op("dve", lambda e: e.tensor_tensor(out=SUF[:], in0=LF[:], in1=CUM[:], op=ALU.subtract), reads=[gb_], writes=[gb_])
                S.op("dve", lambda e: e.tensor_scalar(out=SUF[:], in0=SUF[:], scalar1=CUM[:, SEQ - 1:SEQ], scalar2=None, op0=ALU.add),
                     reads=[gb_], writes=[gb_])
                for sc in range(16):
                    ss = slice(sc * 128, (sc + 1) * 128)
                    o = self.ps[2][:, sc * 24:(sc + 1) * 24]
                    for i, src in enumerate((XG, CUM, SUF)):
                        S.op("pe", lambda e: e.matmul(o, lhsT=src[:, ss], rhs=dsel[:, i, :], start=i == 0, stop=i == 2),
                             reads=[gb_, cb], writes=[self.psb[2]])
                S.op("dve", lambda e: e.tensor_copy(out=BIAS[:].rearrange("p a b -> p (a b)"), in_=self.ps[2][:, 0:384]),
                     reads=[self.psb[2]], writes=[biasb])
                S.barrier()
            selr = sb("selr", [48, 24, 128], F32)
            tri = sb("tri", [128, 2, 128], F32)
            cb2 = Buf()
            S.dma("sp", lambda e: e.dma_start(out=selr[:], in_=T["c_selr"][:, :, :]), writes=[cb2])
            S.dma("sp", lambda e: e.dma_start(out=tri[:], in_=T["c_tri"][:, :, :]), writes=[cb2])
            G = sb("Gn", [128, 1536], F32)
            Gb = Buf()
            S.dma("sp", lambda e: e.dma_start(out=G[:], in_=T["mlstm_head_norm"].partition_broadcast(128)), writes=[Gb])
            wts = Ring([sb("aw%d" % i, [128, 16, 128], BF16) for i in range(6)])
            QT = sb("QT", [128, SEQ], BF16)
            KT = sb("KT", [128, SEQ], BF16)
            QTb, KTb = Buf(), Buf()
            V = sb("Vaug", [128, 16, 144], BF16)
            Vb = Buf()
            S.op("pool", lambda e: e.memset(V[:], 1.0), writes=[Vb])
            OG = sb("OG", [128, 16, 128], BF16)
            OGb = Buf()
            ring_p = Ring([sb("P%d" % i, [128, 512], BF16) for i in range(3)])
            wring = Ring([sb("Wt%d" % i, [128, 512], F32) for i in range(3)])
            Rb = sb("Rb", [128, SEQ], F32)
            Rbb = Buf()
            HS = sb("HS", [128, 16, 128], F32)
            HSb = Buf()
            HD = sb("HD", [128, 128], F32)
            dn = sb("m_dn", [128, 1], F32)
            dn0 = sb("m_dn0", [128, 1], F32)
            smb = Buf()
            sq = sb("a_sq", [128, 16, 128], F32)
            ms = sb("a_ms", [128, 16], F32)
            mixh = sb("mixh", [128, 16, 128], BF16)
            mixhb = Buf()
            for h in range(12):
                for qi, (dst, dstb, scl) in enumerate(((QT, QTb, 1.0), (KT, KTb, 128 ** -0.5))):
                    w, wb_ = wts.next()
                    c = qi * 1536 + h * 128
                    self.load_w(w[:], wb_, W[:, :, c:c + 128])
                    for jg in range(4):
                        self.proj_fm(w, wb_, hT, hTb, jg, jg % 2)
                        S.op("act", lambda e: e.activation(out=dst[:, jg * 512:(jg + 1) * 512], in_=self.ps[jg % 2][:, :],
                                                           func=AF.Copy, scale=scl), reads=[self.psb[jg % 2]], writes=[dstb])
                w, wb_ = wts.next()
                self.load_w(w[:], wb_, W[:, :, 2 * 1536 + h * 128:2 * 1536 + (h + 1) * 128])

                def evac_v(g, pst, pstb):
                    S.op("act", lambda e: e.activation(out=V[:, 4 * g:4 * g + 4, 0:128],
                                                       in_=pst[:].rearrange("p (i d) -> p i d", i=4), func=AF.Copy),
                         reads=[pstb], writes=[Vb])

                self.proj_tm(w, wb_, hT, hTb, evac_v)
                w, wb_ = wts.next()
                self.load_w(w[:], wb_, W[:, :, 3 * 1536 + h * 128:3 * 1536 + (h + 1) * 128])

                def evac_o(g, pst, pstb):
                    S.op("act", lambda e: e.activation(out=OG[:, 4 * g:4 * g + 4, :],
                                                       in_=pst[:].rearrange("p (i d) -> p i d", i=4), func=AF.Sigmoid),
                         reads=[pstb], writes=[OGb])

                self.proj_tm(w, wb_, hT, hTb, evac_o)
                for di, dr in enumerate(("f", "b")):
                    src = CUM if dr == "f" else SUF
                    col = h if dr == "f" else 12 + h
                    for jg in range(4):
                        k = 6 + jg % 2
                        S.op("pe", lambda e: e.matmul(self.ps[k][:, :], lhsT=selr[:, col, :], rhs=src[:, jg * 512:(jg + 1) * 512],
                                                      start=True, stop=True), reads=[gb_, cb2], writes=[self.psb[k]])
                        S.op("dve", lambda e: e.tensor_copy(out=Rb[:, jg * 512:(jg + 1) * 512], in_=self.ps[k][:, :]),
                             reads=[self.psb[k]], writes=[Rbb])

                    def consume(jc, acc, accb, di=di):
                        S.op("dve", lambda e: e.tensor_copy(out=dn0[:], in_=acc[:, 128:129]), reads=[accb], writes=[smb])
                        S.op("dve", lambda e: e.scalar_tensor_tensor(out=dn[:], in0=dn0[:], scalar=-1.0, in1=dn0[:], op0=ALU.mult,
                                                                     op1=ALU.max), reads=[smb], writes=[smb])
                        S.op("dve", lambda e: e.tensor_scalar(out=dn[:], in0=dn[:], scalar1=1.0, scalar2=None, op0=ALU.max),
                             reads=[smb], writes=[smb])
                        S.op("dve", lambda e: e.reciprocal(out=dn[:], in_=dn[:]), reads=[smb], writes=[smb])
                        if di == 0:
                            S.op("dve", lambda e: e.tensor_scalar(out=HS[:, jc, :], in0=acc[:, 0:128], scalar1=dn[:, 0:1], scalar2=None,
                                                                  op0=ALU.mult), reads=[accb, smb], writes=[HSb])
                        else:
                            S.op("dve", lambda e: e.scalar_tensor_tensor(out=HS[:, jc, :], in0=acc[:, 0:128], scalar=dn[:, 0:1],
                                                                         in1=HS[:, jc, :], op0=ALU.mult, op1=ALU.add),
                                 reads=[accb, smb, HSb], writes=[HSb])

                    mode = dict(dir=dr, wring=wring, Rb=Rb, Rbb=Rbb, bias=BIAS, biasb=biasb, col=col, tri=tri[:, di, :], trib=cb2)
                    self.attn_core(QT, QTb, KT, KTb, (0, 128), 16, lambda sc: V[:, sc, 0:130], [Vb], 1.0, consume, ring_p, mode=mode)
                self.head_norm_out(HS, HSb, sq, ms, G[:, h * 128:(h + 1) * 128], Gb, 1.0, (OG, OGb), mixh, mixhb, smb)
                S.dma("sp", lambda e: e.dma_start(out=mix_d.rearrange("(jc p) f -> p jc f", p=128)[:, :, h * 128:(h + 1) * 128],
                                                  in_=mixh[:]), reads=[mixhb])
            self.mem_heads(pes, W, 4 * 1536, hT, hTb, mix_d, wts, ring_p, QT, QTb, mixh, mixhb)
            self.bg_finish()
            S.barrier()

    def phase_linear(self, mix_d, w_ap, h_d, g_ap, b_ap, out_d):
        S, T = self.S, self.T
        with ExitStack() as pes:
            sb = lambda n, s, d: self.sbuf(pes, n, s, d)
            Wb = sb("Wout", [128, 16, D], BF16)
            Wbb = Buf()
            src = w_ap.rearrange("(dc p) n -> p dc n", p=128)
            for i in range(4):
                self.load_w(Wb[:, :, i * 512:(i + 1) * 512], Wbb, src[:, :, i * 512:(i + 1) * 512])
            gt, bt, gbb, st, stb = self.load_ln(pes, g_ap, b_ap)
            mx = Ring([sb("lmx%d" % i, [128, D], BF16) for i in range(2)])
            mT = Ring([sb("lmT%d" % i, [128, 16, 128], BF16) for i in range(2)])
            hx = Ring([sb("lhx%d" % i, [128, D], F32) for i in range(2)])
            zr = Ring([sb("lz%d" % i, [128, D], F32) for i in range(2)])
            for tt in range(NT):
                ts = slice(tt * 128, (tt + 1) * 128)
                m, mb = mx.next()
                S.dma("sp", lambda e: e.dma_start(out=m[:], in_=mix_d[ts, :]), writes=[mb])
                x, xb_ = hx.next()
                S.dma("sp", lambda e: e.dma_start(out=x[:], in_=h_d[ts, :]), writes=[xb_])
                t, tb = mT.next()
                self.transpose_tile(m, mb, t[:], [tb], tt)
                z, zb = zr.next()
                for nb in range(4):
                    k = [0, 1, 2, 3][nb] if tt % 2 == 0 else [4, 5, 2, 3][nb]
                    for c in range(16):
                        S.op("pe", lambda e: e.matmul(self.ps[k][:, :], lhsT=t[:, c, :], rhs=Wb[:, c, nb * 512:(nb + 1) * 512],
                                                      start=c == 0, stop=c == 15), reads=[tb, Wbb], writes=[self.psb[k]])
                    S.op("dve", lambda e: e.scalar_tensor_tensor(out=z[:, nb * 512:(nb + 1) * 512], in0=x[:, nb * 512:(nb + 1) * 512],
                                                                 scalar=ALPHA, in1=self.ps[k][:, :], op0=ALU.mult, op1=ALU.add),
                         reads=[xb_, self.psb[k]], writes=[zb])
                self.ln_store(z, zb, gt, bt, gbb, out_d[ts, :], st, stb)
            S.barrier()

    def conv_jobs(self, es, tables):
        S = self.S
        fr = Ring([self.sbuf(es, "bgf%d" % i, [128, 1024], F32) for i in range(2)])
        br = Ring([self.sbuf(es, "bgb%d" % i, [128, 1024], BF16) for i in range(2)])
        jobs = []
        for src, dst in tables:
            sv = src.rearrange("(p r) d -> p r d", p=128)
            dv = dst.rearrange("(p r) d -> p r d", p=128)
            for r in range(128):
                for hf in range(2):
                    jobs.append((sv[:, r, hf * 1024:(hf + 1) * 1024], dv[:, r, hf * 1024:(hf + 1) * 1024]))
        self.bg_total = len(jobs)
        loaded = []

        def load(i):
            f, fb = fr.next()
            S.dma("sp", lambda e: e.dma_start(out=f[:], in_=jobs[i][0]), writes=[fb])
            loaded.append((f, fb))

        load(0)
        for i in range(len(jobs)):
            if i + 1 < len(jobs):
                load(i + 1)
            f, fb = loaded[i]
            b, bb = br.next()
            S.op("pool", lambda e: e.tensor_copy(out=b[:], in_=f[:]), reads=[fb], writes=[bb])
            S.dma("sp", lambda e: e.dma_start(out=jobs[i][1], in_=b[:]), reads=[bb])
            yield

    def bg_start(self, es, tables, nsteps):
        self.bg = self.conv_jobs(es, tables)
        self.bg_acc = 0.0
        self.bg_rate = (len(tables) * 256) / float(nsteps)

    def bg_tick(self):
        if getattr(self, "bg", None) is None:
            return
        self.bg_acc += self.bg_rate
        while self.bg_acc >= 1.0:
            self.bg_acc -= 1.0
            if next(self.bg, "done") == "done":
                self.bg = None
                return

    def bg_finish(self):
        if getattr(self, "bg", None) is not None:
            for _ in self.bg:
                pass
            self.bg = None

    def phase_convert(self, tables):
        S = self.S
        with ExitStack() as pes:
            fr = Ring([self.sbuf(pes, "cvf%d" % i, [128, 4096], F32) for i in range(3)])
            br = Ring([self.sbuf(pes, "cvb%d" % i, [128, 4096], BF16) for i in range(3)])
            jobs = []
            for src, dst in tables:
                sv = src.rearrange("(p r) d -> p r d", p=128)
                dv = dst.rearrange("(p r) d -> p r d", p=128)
                for c in range(64):
                    jobs.append((sv[:, 2 * c:2 * c + 2, :], dv[:, 2 * c:2 * c + 2, :]))
            loaded = []

            def load(i):
                f, fb = fr.next()
                S.dma("sp", lambda e: e.dma_start(out=f[:].rearrange("p (r d) -> p r d", r=2), in_=jobs[i][0]), writes=[fb])
                loaded.append((f, fb))

            for i in range(min(2, len(jobs))):
                load(i)
            for i in range(len(jobs)):
                if i + 2 < len(jobs):
                    load(i + 2)
                f, fb = loaded[i]
                b, bb = br.next()
                eng = ("act", "dve", "pool")[i % 3]
                if eng == "act":
                    S.op("act", lambda e: e.activation(out=b[:], in_=f[:], func=AF.Copy), reads=[fb], writes=[bb])
                else:
                    S.op(eng, lambda e: e.tensor_copy(out=b[:], in_=f[:]), reads=[fb], writes=[bb])
                S.dma("sp", lambda e: e.dma_start(out=jobs[i][1], in_=b[:].rearrange("p (r d) -> p r d", r=2)), reads=[bb])
            S.barrier()

    def phase_peer(self, li, h_d, out_d, ub_d, vb_d):
        S, T = self.S, self.T
        with ExitStack() as pes:
            sb = lambda n, s, d: self.sbuf(pes, n, s, d)
            Wq = sb("Wq", [128, 16, D], BF16)
            Wqb = Buf()
            src = T["peer_w_query"][li].rearrange("(dc p) n -> p dc n", p=128)
            for i in range(4):
                self.load_w(Wq[:, :, i * 512:(i + 1) * 512], Wqb, src[:, :, i * 512:(i + 1) * 512])
            keysT = sb("keysT", [128, 16, 128], BF16)
            kb = Buf()
            self.load_w(keysT[:], kb, T["peer_keysT"][li])
            gt, bt, gbb, st, stb = self.load_ln(pes, T["ln_ffn_g"][li:li + 1, :], T["ln_ffn_b"][li:li + 1, :])
            iota = sb("iota", [128, 256], F32)
            ib = Buf()
            S.dma("sp", lambda e: e.dma_start(out=iota[:], in_=T["c_iota"][:, :]), writes=[ib])
            hx = Ring([sb("px%d" % i, [128, D], F32) for i in range(3)])
            xbf_r = Ring([sb("pxb%d" % i, [128, D], BF16) for i in range(2)])
            hTt = sb("phT", [128, 16, 128], BF16)
            hTtb = Buf()
            qT = sb("pqT", [128, 16, 128], BF16)
            qTb = Buf()
            Ssb = sb("pS", [128, 16, 128], F32)
            Sb = Buf()
            tmpa = sb("ptmp", [128, 256], F32)
            v16 = sb("pv16", [128, 2, 16], F32)
            i16u = sb("pi16u", [128, 2, 16], U32)
            i16 = sb("pi16", [128, 2, 16], F32)
            cand = sb("pcand", [128, 16, 16], F32)
            cidx = sb("pcidx", [128, 16, 16], F32)
            c16 = sb("pc16", [128, 16], F32)
            p16u = sb("pp16u", [128, 16], U32)
            p16 = sb("pp16", [128, 16], F32)
            g16 = sb("pg16", [128, 16, 16], F32)
            ab16 = sb("pab16", [128, 2, 16], F32)
            e2 = sb("pe2", [128, 2, 16], F32)
            thr = sb("pthr", [128, 16], F32)
            S.op("dve", lambda e: e.tensor_single_scalar(out=thr[:], in_=iota[:, 1:17], scalar=16.0, op=ALU.mult), reads=[ib], writes=[ib])
            tkb = Buf()
            eidf = sb("peidf", [128, 128], F32)
            gsum = sb("pgsum", [128, 8], F32)
            nmax = sb("pnmax", [128, 8], F32)
            eidx_r = Ring([sb("peidx%d" % i, [128, 128], I32) for i in range(3)])
            gate_r = Ring([sb("pgate%d" % i, [128, 128], F32) for i in range(3)])
            rows = Ring([sb("prow%d" % i, [128, D], BF16) for i in range(10)])
            dgr = Ring([sb("pdg%d" % i, [128, 128], BF16) for i in range(4)])
            junk_r = Ring([sb("pjunk%d" % i, [128, D], BF16) for i in range(2)])
            z = sb("pz", [128, D], F32)
            zb = Buf()

            def gen_a(tt, ctx):
                ts = slice(tt * 128, (tt + 1) * 128)
                x, xb_ = hx.next()
                eidx, eb = eidx_r.next()
                gate, _ = gate_r.next()
                xbf, xbfb = xbf_r.next()
                S.dma("sp", lambda e: e.dma_start(out=x[:], in_=h_d[ts, :]), writes=[xb_])
                S.op("act", lambda e: e.activation(out=xbf[:], in_=x[:], func=AF.Copy), reads=[xb_], writes=[xbfb])
                self.transpose_tile(xbf, xbfb, hTt[:], [hTtb], tt, banks=(2, 3))
                yield
                for half in range(2):
                    for hq in range(8):
                        hp = half * 8 + hq
                        k = hq // 4
                        for dc in range(16):
                            S.op("pe", lambda e: e.matmul(self.ps[k][:, (hq % 4) * 128:(hq % 4 + 1) * 128],
                                                          lhsT=Wq[:, dc, hp * 128:(hp + 1) * 128], rhs=hTt[:, dc, :],
                                                          start=dc == 0, stop=dc == 15), reads=[Wqb, hTtb], writes=[self.psb[k]])
                    for k in range(2):
                        S.op("act", lambda e: e.activation(out=qT[:, half * 8 + 4 * k:half * 8 + 4 * k + 4, :],
                                                           in_=self.ps[k][:].rearrange("p (a b) -> p a b", a=4), func=AF.Copy),
                             reads=[self.psb[k]], writes=[qTb])
                    yield
                for half in range(2):
                    for hq in range(8):
                        hp = half * 8 + hq
                        k = hq // 4
                        S.op("pe", lambda e: e.matmul(self.ps[k][:, (hq % 4) * 128:(hq % 4 + 1) * 128], lhsT=qT[:, hp, :],
                                                      rhs=keysT[:, hp, :], start=True, stop=True), reads=[qTb, kb], writes=[self.psb[k]])
                    for k in range(2):
                        S.op("act", lambda e: e.activation(out=Ssb[:, half * 8 + 4 * k:half * 8 + 4 * k + 4, :],
                                                           in_=self.ps[k][:].rearrange("p (a b) -> p a b", a=4), func=AF.Copy),
                             reads=[self.psb[k]], writes=[Sb])
                    yield
                for h in range(8):
                    for p in range(2):
                        s_ap = Ssb[:, 2 * h + p, :]
                        S.op("dve", lambda e: e.max(out=v16[:, p, 0:8], in_=s_ap), reads=[Sb], writes=[tkb])
                        S.op("dve", lambda e: e.max_index(out=i16u[:, p, 0:8], in_max=v16[:, p, 0:8], in_values=s_ap), reads=[Sb, tkb], writes=[tkb])
                        S.op("dve", lambda e: e.match_replace(out=tmpa[:, 0:128], in_to_replace=v16[:, p, 0:8], in_values=s_ap,
                                                              imm_value=NEG), reads=[Sb, tkb], writes=[tkb])
                        S.op("dve", lambda e: e.max(out=v16[:, p, 8:16], in_=tmpa[:, 0:128]), reads=[tkb], writes=[tkb])
                        S.op("dve", lambda e: e.max_index(out=i16u[:, p, 8:16], in_max=v16[:, p, 8:16], in_values=tmpa[:, 0:128]),
                             reads=[tkb], writes=[tkb])
                    S.op("dve", lambda e: e.tensor_copy(out=i16[:], in_=i16u[:]), reads=[tkb], writes=[tkb])
                    S.op("dve", lambda e: e.tensor_single_scalar(out=i16[:, 0, :], in_=i16[:, 0, :], scalar=128.0, op=ALU.mult),
                         reads=[tkb], writes=[tkb])
                    S.op("dve", lambda e: e.tensor_tensor(out=cand[:], in0=v16[:, 0, :].unsqueeze(2).to_broadcast([128, 16, 16]),
                                                          in1=v16[:, 1, :].unsqueeze(1).to_broadcast([128, 16, 16]), op=ALU.add),
                         reads=[tkb], writes=[tkb])
                    cf = cand[:].rearrange("p a b -> p (a b)")
                    S.op("dve", lambda e: e.max(out=c16[:, 0:8], in_=cf), reads=[tkb], writes=[tkb])
                    S.op("dve", lambda e: e.match_replace(out=tmpa[:], in_to_replace=c16[:, 0:8], in_values=cf, imm_value=NEG),
                         reads=[tkb], writes=[tkb])
                    S.op("dve", lambda e: e.max(out=c16[:, 8:16], in_=tmpa[:]), reads=[tkb], writes=[tkb])
                    S.op("dve", lambda e: e.max_index(out=p16u[:, 0:8], in_max=c16[:, 0:8], in_values=cf), reads=[tkb], writes=[tkb])
                    S.op("dve", lambda e: e.max_index(out=p16u[:, 8:16], in_max=c16[:, 8:16], in_values=tmpa[:]), reads=[tkb], writes=[tkb])
                    S.op("dve", lambda e: e.tensor_copy(out=p16[:], in_=p16u[:]), reads=[tkb], writes=[tkb])
                    S.op("dve", lambda e: e.tensor_tensor(out=g16[:], in0=p16[:].unsqueeze(2).to_broadcast([128, 16, 16]),
                                                          in1=thr[:].unsqueeze(1).to_broadcast([128, 16, 16]), op=ALU.is_ge),
                         reads=[tkb, ib], writes=[tkb])
                    S.op("dve", lambda e: e.tensor_reduce(out=ab16[:, 0, :], in_=g16[:], axis=AX.X, op=ALU.add), reads=[tkb], writes=[tkb])
                    S.op("dve", lambda e: e.scalar_tensor_tensor(out=ab16[:, 1, :], in0=ab16[:, 0, :], scalar=-16.0, in1=p16[:],
                                                                 op0=ALU.mult, op1=ALU.add), reads=[tkb], writes=[tkb])
                    for p in range(2):
                        S.op("dve", lambda e: e.tensor_tensor(out=g16[:], in0=iota[:, 0:16].unsqueeze(1).to_broadcast([128, 16, 16]),
                                                              in1=ab16[:, p, :].unsqueeze(2).to_broadcast([128, 16, 16]), op=ALU.is_equal),
                             reads=[tkb, ib], writes=[tkb])
                        S.op("dve", lambda e: e.tensor_tensor(out=g16[:], in0=g16[:],
                                                              in1=i16[:, p, :].unsqueeze(1).to_broadcast([128, 16, 16]), op=ALU.mult),
                             reads=[tkb], writes=[tkb])
                        S.op("dve", lambda e: e.tensor_reduce(out=e2[:, p, :], in_=g16[:], axis=AX.X, op=ALU.add), reads=[tkb], writes=[tkb])
                    S.op("dve", lambda e: e.tensor_tensor(out=eidf[:, h * 16:(h + 1) * 16], in0=e2[:, 0, :], in1=e2[:, 1, :], op=ALU.add),
                         reads=[tkb], writes=[tkb])
                    S.op("dve", lambda e: e.tensor_single_scalar(out=nmax[:, h:h + 1], in_=c16[:, 0:1], scalar=-1.0, op=ALU.mult),
                         reads=[tkb], writes=[tkb])
                    S.op("act", lambda e: e.activation(out=gate[:, h * 16:(h + 1) * 16], in_=c16[:], func=AF.Exp, bias=nmax[:, h:h + 1],
                                                       scale=1.0, accum_out=gsum[:, h:h + 1]), reads=[tkb], writes=[eb, tkb])
                    yield
                S.op("dve", lambda e: e.reciprocal(out=gsum[:], in_=gsum[:]), reads=[tkb], writes=[tkb])
                S.op("dve", lambda e: e.tensor_tensor(out=gate[:].rearrange("p (h k) -> p h k", h=8),
                                                      in0=gate[:].rearrange("p (h k) -> p h k", h=8),
                                                      in1=gsum[:].unsqueeze(2).to_broadcast([128, 8, 16]), op=ALU.mult),
                     reads=[eb, tkb], writes=[eb])
                S.op("dve", lambda e: e.tensor_copy(out=eidx[:], in_=eidf[:]), reads=[tkb, eb], writes=[eb])
                ctx.update(x=x, xb=xb_, eidx=eidx, gate=gate, eb=eb, xbf=xbf, xbfb=xbfb)
                yield

            dots_r = Ring([sb("pdots%d" % i, [128, 128], F32) for i in range(2)])
            wgt_r = Ring([sb("pwgt%d" % i, [128, 128], F32) for i in range(2)])

            def u_start(c):
                c["dots"], c["db"] = dots_r.next()
                c["wgt"], _ = wgt_r.next()

            def u_slot(c, sl):
                r, rb_ = rows.next()
                S.dma("pool", lambda e: e.indirect_dma_start(out=r[:], out_offset=None, in_=ub_d[:, :],
                                                             in_offset=bass.IndirectOffsetOnAxis(ap=c["eidx"][:, sl:sl + 1], axis=0)),
                      reads=[c["eb"]], writes=[rb_])
                pr, prb = junk_r.next()
                S.op("dve", lambda e: e.tensor_tensor(out=pr[:], in0=r[:], in1=c["xbf"][:], op=ALU.mult), reads=[rb_, c["xbfb"]], writes=[prb])
                S.op("act", lambda e: e.activation(out=pr[:], in_=pr[:], func=AF.Copy, accum_out=c["dots"][:, sl:sl + 1]),
                     reads=[prb], writes=[prb, c["db"]])

            def u_fin(c):
                S.op("act", lambda e: e.activation(out=c["wgt"][:], in_=c["dots"][:], func=AF.Gelu), reads=[c["db"]], writes=[c["db"]])
                S.op("dve", lambda e: e.tensor_tensor(out=c["wgt"][:], in0=c["wgt"][:], in1=c["gate"][:], op=ALU.mult),
                     reads=[c["db"], c["eb"]], writes=[c["db"]])

            def v_slot(c, sl):
                r, rb_ = rows.next()
                S.dma("pool", lambda e: e.indirect_dma_start(out=r[:], out_offset=None, in_=vb_d[:, :],
                                                             in_offset=bass.IndirectOffsetOnAxis(ap=c["eidx"][:, sl:sl + 1], axis=0)),
                      reads=[c["eb"]], writes=[rb_])
                dg, dgb = dgr.next()
                S.op("act", lambda e: e.activation(out=dg[:], in_=self.ident[:], func=AF.Copy, scale=c["wgt"][:, sl:sl + 1]),
                     reads=[c["db"], self.ident_b], writes=[dgb])
                for nb in range(4):
                    S.op("pe", lambda e: e.matmul(self.ps[4 + nb][:, :], lhsT=dg[:], rhs=r[:, nb * 512:(nb + 1) * 512],
                                                  start=sl == 0, stop=sl == 127), reads=[dgb, rb_], writes=[self.psb[4 + nb]])

            def v_fin(c, tt):
                for nb in range(4):
                    S.op("dve", lambda e: e.scalar_tensor_tensor(out=z[:, nb * 512:(nb + 1) * 512], in0=c["x"][:, nb * 512:(nb + 1) * 512],
                                                                 scalar=ALPHA, in1=self.ps[4 + nb][:, :], op0=ALU.mult, op1=ALU.add),
                         reads=[c["xb"], self.psb[4 + nb]], writes=[zb])
                self.ln_store(z, zb, gt, bt, gbb, out_d[tt * 128:(tt + 1) * 128, :], st, stb)

            ctxs = [dict() for _ in range(NT)]
            for _ in gen_a(0, ctxs[0]):
                pass
            for s_ in range(-1, NT):
                tv, tu, ta = s_, s_ + 1, s_ + 2
                gen = gen_a(ta, ctxs[ta]) if ta < NT else None
                if tu < NT:
                    u_start(ctxs[tu])
                for sl0 in range(0, 128, GRP):
                    if tv >= 0:
                        for sl in range(sl0, sl0 + GRP):
                            v_slot(ctxs[tv], sl)
                    if tu < NT:
                        for sl in range(sl0, sl0 + GRP):
                            u_slot(ctxs[tu], sl)
                    if gen is not None:
                        for _ in range(GRP // 8):
                            next(gen, None)
                if gen is not None:
                    for _ in gen:
                        pass
                if tv >= 0:
                    v_fin(ctxs[tv], tv)
                if tu < NT:
                    u_fin(ctxs[tu])
            S.barrier()


IN_SPECS = [
    ("x", [SEQ, D], F32), ("mem", [NMEM, D], F32), ("pos", [1, SEQ], I32),
    ("attn_w_in", [D, 5120], F32), ("attn_w_rot", [D, 3072], F32), ("attn_lambda", [1, 256], F32),
    ("attn_head_norm", [1, 1536], F32), ("mlstm_w_in", [D, 6704], F32), ("gate_bias", [48, 1], F32),
    ("mlstm_head_norm", [1, 1536], F32), ("mem_w_kv", [D, 1024], F32), ("w_out", [2, D, D], F32),
    ("ln_mix_g", [2, D], F32), ("ln_mix_b", [2, D], F32), ("peer_w_query", [2, D, D], F32),
    ("peer_keysT", [2, 128, 16, 128], F32), ("peer_u0", [16384, D], F32), ("peer_u1", [16384, D], F32), ("peer_v0", [16384, D], F32), ("peer_v1", [16384, D], F32),
    ("ln_ffn_g", [2, D], F32), ("ln_ffn_b", [2, D], F32),
    ("c_ident", [128, 128], F32), ("c_rope", [128, 2], F32), ("c_dsel", [48, 3, 24], F32),
    ("c_selr", [48, 24, 128], F32), ("c_tri", [128, 2, 128], F32), ("c_iota", [128, 256], F32),
]


def build_nc(phases=("mem", "attn", "lin0", "peer0", "mlstm", "lin1", "peer1"), debug_out=None):
    nc = bass.Bass("TRN2", target_bir_lowering=False)
    T = {}
    for name, shape, dt in IN_SPECS:
        T[name] = nc.dram_tensor(name, shape, dt, kind="ExternalInput").ap()
    out = nc.dram_tensor("out", [SEQ, D], F32, kind="ExternalOutput").ap()
    h1 = nc.dram_tensor("h1", [SEQ, D], F32, kind="Internal").ap()
    h2 = nc.dram_tensor("h2", [SEQ, D], F32, kind="Internal").ap()
    h3 = nc.dram_tensor("h3", [SEQ, D], F32, kind="Internal").ap()
    mixk = "ExternalOutput" if debug_out == "mix" else "Internal"
    mix_d = nc.dram_tensor("mix_d", [SEQ, D], BF16, kind=mixk).ap()
    tb = {}
    for nm in ("u0", "v0", "u1", "v1"):
        tb[nm] = nc.dram_tensor("tb_" + nm, [16384, D], BF16, kind="Internal").ap()
    with ExitStack() as es:
        S = Sched(nc, es)
        B = Builder(nc, es, S, T)
        ph = list(phases)
        cv = lambda li: [(T["peer_%s%d" % (k, li)], tb["%s%d" % (k, li)]) for k in ("u", "v")]
        conv = []
        if "peer0" in ph and "attn" not in ph:
            conv += cv(0)
        if "peer1" in ph and "mlstm" not in ph:
            conv += cv(1)
        if conv:
            B.phase_convert(conv)
        last = {"lin0": h1, "peer0": h2, "lin1": h3, "peer1": out}
        dst = lambda p: out if p == ph[-1] else last[p]
        if "mem" in ph:
            B.phase_mem()
        if "attn" in ph:
            B.phase_attn(T["x"], mix_d, cv(0) if "peer0" in ph else None)
        if "lin0" in ph:
            B.phase_linear(mix_d, T["w_out"][0], T["x"], T["ln_mix_g"][0:1, :], T["ln_mix_b"][0:1, :], dst("lin0"))
        if "peer0" in ph:
            B.phase_peer(0, h1 if "lin0" in ph else T["x"], dst("peer0"), tb["u0"], tb["v0"])
        if "mlstm" in ph:
            B.phase_mlstm(h2 if "peer0" in ph else T["x"], mix_d, cv(1) if "peer1" in ph else None)
        if "lin1" in ph:
            B.phase_linear(mix_d, T["w_out"][1], h2 if "peer0" in ph else T["x"], T["ln_mix_g"][1:2, :], T["ln_mix_b"][1:2, :], dst("lin1"))
        if "peer1" in ph:
            B.phase_peer(1, h3 if "lin1" in ph else T["x"], dst("peer1"), tb["u1"], tb["v1"])
        S.finish()
        print("instructions:", S.nops)
    return nc


def host_consts():
    c = {}
    c["c_ident"] = np.eye(128, dtype=np.float32)
    p = np.arange(128)
    f = (p % 32).astype(np.float64)
    inv = np.power(10000.0, -f * (2.0 / 64.0))
    rope = np.zeros((128, 2), np.float32)
    rope[:, 0] = (inv.astype(np.float32).astype(np.float64) / (2.0 * np.pi)).astype(np.float32)
    rope[:, 1] = np.where((p % 64) < 32, -1.0, 1.0)
    c["c_rope"] = rope
    dsel = np.zeros((48, 3, 24), np.float32)
    for n in range(12):
        dsel[n, 0, n] = 1.0
        dsel[24 + n, 0, 12 + n] = 1.0
        dsel[12 + n, 1, n] = -1.0
        dsel[36 + n, 2, 12 + n] = -1.0
    c["c_dsel"] = dsel
    selr = np.zeros((48, 24, 128), np.float32)
    for n in range(12):
        selr[12 + n, n, :] = 1.0
        selr[36 + n, 12 + n, :] = 1.0
    c["c_selr"] = selr
    s = np.arange(128)[:, None]
    j = np.arange(128)[None, :]
    tri = np.zeros((128, 2, 128), np.float32)
    tri[:, 0, :] = (s <= j)
    tri[:, 1, :] = (s >= j)
    c["c_tri"] = tri
    c["c_iota"] = np.tile(np.arange(256, dtype=np.float32)[None, :], (128, 1))
    return c


def host_shared(inp):
    sh = dict(host_consts())
    w = np.ascontiguousarray(inp["attn_w_in"][0])
    sh["attn_w_in"] = w
    cols = np.arange(3072)
    r = cols % 64
    partner = np.where(r < 32, cols + 32, cols - 32)
    sh["attn_w_rot"] = np.ascontiguousarray(w[:, partner])
    sh["attn_lambda"] = np.ascontiguousarray(inp["attn_lambda"][0].reshape(1, 256))
    sh["attn_head_norm"] = np.ascontiguousarray(inp["attn_head_norm"][0].reshape(1, 1536))
    sh["mlstm_w_in"] = np.ascontiguousarray(inp["mlstm_w_in"][0])
    sh["gate_bias"] = np.ascontiguousarray(inp["mlstm_gate_bias"][0].reshape(48, 1))
    sh["mlstm_head_norm"] = np.ascontiguousarray(inp["mlstm_head_norm"][0].reshape(1, 1536))
    for k in ("mem_w_kv", "w_out", "ln_mix_g", "ln_mix_b", "peer_w_query", "ln_ffn_g", "ln_ffn_b"):
        sh[k] = np.ascontiguousarray(inp[k])
    for li in range(2):
        sh["peer_u%d" % li] = np.ascontiguousarray(inp["peer_u"][li])
        sh["peer_v%d" % li] = np.ascontiguousarray(inp["peer_v"][li])
    sk = inp["peer_sub_keys"].reshape(2, 16, 128, 128)
    sh["peer_keysT"] = np.ascontiguousarray(sk.transpose(0, 3, 1, 2))
    return {k: np.asarray(v, dtype=(np.int32 if k == "pos" else np.float32)) for k, v in sh.items()}


def kernel(**inputs):
    inp = {k: np.asarray(v) for k, v in inputs.items()}
    shared = host_shared(inp)
    nc = build_nc()
    in_maps = []
    for b in range(8):
        m = dict(shared)
        m["x"] = np.ascontiguousarray(inp["x"][b], dtype=np.float32)
        m["mem"] = np.ascontiguousarray(inp["mem"][b], dtype=np.float32)
        m["pos"] = np.ascontiguousarray(inp["positions"][b].reshape(1, SEQ), dtype=np.int32)
        in_maps.append(m)
    res = run_bass_kernel_spmd(nc, in_maps, core_ids=list(range(8)))
    return np.stack([np.asarray(r["out"], dtype=np.float32) for r in res.results], axis=0)
```

```python
import math
from contextlib import ExitStack
import numpy as np
import concourse.bass as bass
import concourse.mybir as mybir
from concourse.bass_utils import run_bass_kernel_spmd

F32 = mybir.dt.float32
BF16 = mybir.dt.bfloat16
I32 = mybir.dt.int32
U32 = mybir.dt.uint32
AF = mybir.ActivationFunctionType
ALU = mybir.AluOpType
AX = mybir.AxisListType

D = 2048
SEQ = 2048
NT = SEQ // 128
NMEM = 256
DEPTH = 2
ALPHA = (2.0 * DEPTH) ** 0.25
EPS = 1e-5
NDMA_SLOTS = 12
NEG = -1.0e30
GRP = 8


class Buf:
    __slots__ = ("w", "r")

    def __init__(self):
        self.w = None
        self.r = {}


def bufs(n):
    return [Buf() for _ in range(n)]


class Sched:
    COMPUTE = ("pe", "act", "dve", "pool")

    def __init__(self, nc, es):
        self.nc = nc
        self.h = {"pe": nc.tensor, "act": nc.scalar, "dve": nc.vector, "pool": nc.gpsimd, "sp": nc.sync}
        self.sem = {}
        self.cnt = {}
        self.seen = {e: {} for e in self.h}
        for e in self.COMPUTE:
            self.sem[e] = es.enter_context(nc.semaphore("s_" + e))
            self.cnt[e] = 0
        self.dma_slots = {}
        for qn in ("sp", "pool", "act"):
            self.dma_slots[qn] = []
            for i in range(NDMA_SLOTS):
                nm = "d_%s%d" % (qn, i)
                self.sem[nm] = es.enter_context(nc.semaphore(nm))
                self.cnt[nm] = 0
                self.dma_slots[qn].append(nm)
        self.dma_rr = {qn: 0 for qn in self.dma_slots}
        self.nops = 0

    def _mult(self, e):
        return 1 if e in self.COMPUTE else 16

    def _need(self, eng, deps):
        for e, c in deps.items():
            if e == eng and eng == "pe":
                continue
            if self.seen[eng].get(e, 0) < c:
                self.seen[eng][e] = c
                self.h[eng].wait_ge(self.sem[e], c * self._mult(e))

    @staticmethod
    def _deps(reads, writes):
        deps = {}
        for b in reads:
            if b.w is not None:
                e, c = b.w
                if deps.get(e, 0) < c:
                    deps[e] = c
        for b in writes:
            if b.w is not None:
                e, c = b.w
                if deps.get(e, 0) < c:
                    deps[e] = c
            for e, c in b.r.items():
                if deps.get(e, 0) < c:
                    deps[e] = c
        return deps

    @staticmethod
    def _mark(who, c, reads, writes):
        for b in reads:
            if b.r.get(who, 0) < c:
                b.r[who] = c
        for b in writes:
            b.w = (who, c)
            b.r = {}

    def op(self, eng, fn, reads=(), writes=()):
        self._need(eng, self._deps(reads, writes))
        self.cnt[eng] += 1
        fn(self.h[eng]).then_inc(self.sem[eng], 1)
        self._mark(eng, self.cnt[eng], reads, writes)
        self.nops += 1

    def dma(self, qn, fn, reads=(), writes=()):
        slots = self.dma_slots[qn]
        slot = slots[self.dma_rr[qn] % len(slots)]
        self.dma_rr[qn] += 1
        deps = self._deps(reads, writes)
        if self.cnt[slot] > 0:
            deps[slot] = max(deps.get(slot, 0), self.cnt[slot])
        self._need(qn, deps)
        self.cnt[slot] += 1
        fn(self.h[qn]).then_inc(self.sem[slot], 16)
        self._mark(slot, self.cnt[slot], reads, writes)
        self.nops += 1

    def barrier(self):
        allc = {e: c for e, c in self.cnt.items() if c > 0}
        for eng in self.h:
            self._need(eng, dict(allc))

    def finish(self):
        for qn, slots in self.dma_slots.items():
            self._need(qn, {s: self.cnt[s] for s in slots if self.cnt[s] > 0})


class Ring:
    def __init__(self, tiles):
        self.t = tiles
        self.b = bufs(len(tiles))
        self.i = 0

    def next(self):
        k = self.i % len(self.t)
        self.i += 1
        return self.t[k], self.b[k]


class Builder:
    def __init__(self, nc, es, S, T):
        self.nc, self.es, self.S, self.T = nc, es, S, T
        self.ps = [es.enter_context(nc.psum_tensor("ps%d" % i, [128, 512], F32)) for i in range(8)]
        self.psb = bufs(8)
        sb = self.sbuf
        self.ident = sb(es, "ident", [128, 128], BF16)
        self.ident_b = Buf()
        S.dma("pool", lambda e: e.dma_start(out=self.ident[:], in_=T["c_ident"][:, :]), writes=[self.ident_b])
        self.memKT = sb(es, "memKT", [128, 4, 256], BF16)
        self.memKT_b = Buf()
        self.memV = sb(es, "memV", [128, 2, 4, 144], BF16)
        self.memV_b = Buf()
        self.eps_t = sb(es, "eps_t", [128, 1], F32)
        self.eps_b = Buf()
        S.op("pool", lambda e: e.memset(self.eps_t[:], EPS), writes=[self.eps_b])

    def sbuf(self, es, name, shape, dt):
        self.uid = getattr(self, "uid", 0) + 1
        return es.enter_context(self.nc.sbuf_tensor("%s_%d" % (name, self.uid), shape, dt))

    def load_T(self, pes, src, hT, hTb, src_bf16=False):
        S, nc = self.S, self.nc
        xin = Ring([self.sbuf(pes, "ldx%d" % i, [128, D], BF16 if src_bf16 else F32) for i in range(2)])
        xbr = None if src_bf16 else Ring([self.sbuf(pes, "ldb%d" % i, [128, D], BF16) for i in range(2)])
        for tt in range(NT):
            xt, xb_ = xin.next()
            S.dma("sp", lambda e: e.dma_start(out=xt[:], in_=src[tt * 128:(tt + 1) * 128, :]), writes=[xb_])
            if src_bf16:
                xb, xbb = xt, xb_
            else:
                xb, xbb = xbr.next()
                S.op("act", lambda e: e.activation(out=xb[:], in_=xt[:], func=AF.Copy), reads=[xb_], writes=[xbb])
            self.transpose_tile(xb, xbb, hT[:, :, tt * 128:(tt + 1) * 128], [hTb[tt]], tt)

    def transpose_tile(self, xb, xbb, out_ap, out_bufs, parity, banks=(6, 7)):
        S = self.S
        for half in range(2):
            k = banks[(2 * parity + half) % 2]
            pv = self.ps[k][:].bitcast(BF16)
            for c in range(8):
                dc = half * 8 + c
                S.op("pe", lambda e: e.transpose(out=pv[:, c * 128:(c + 1) * 128], in_=xb[:, dc * 128:(dc + 1) * 128],
                                                 identity=self.ident[:]),
                     reads=[xbb, self.ident_b], writes=[self.psb[k]])
            S.op("dve", lambda e: e.tensor_copy(out=out_ap[:, half * 8:(half + 1) * 8, :],
                                                in_=pv.rearrange("p (c t) -> p c t", c=8)),
                 reads=[self.psb[k]], writes=out_bufs)

    def load_w(self, wt, wb_, src_ap, q="pool"):
        self.S.dma(q, lambda e: e.dma_start(out=wt, in_=src_ap), writes=[wb_])

    def ln_store(self, z, zb, gt, bt, gbb, dst_ap, st, stb):
        S = self.S
        stats, mv, sd = st
        for c in range(4):
            S.op("dve", lambda e: e.bn_stats(out=stats[:, c, :], in_=z[:, c * 512:(c + 1) * 512]), reads=[zb], writes=[stb])
        S.op("dve", lambda e: e.bn_aggr(out=mv[:], in_=stats[:].rearrange("p a b -> p (a b)")), reads=[stb], writes=[stb])
        S.op("act", lambda e: e.activation(out=sd[:], in_=mv[:, 1:2], func=AF.Sqrt, bias=self.eps_t[:], scale=1.0),
             reads=[stb, self.eps_b], writes=[stb])
        S.op("dve", lambda e: e.reciprocal(out=sd[:], in_=sd[:]), reads=[stb], writes=[stb])
        S.op("dve", lambda e: e.tensor_scalar(out=z[:], in0=z[:], scalar1=mv[:, 0:1], scalar2=sd[:, 0:1],
                                              op0=ALU.subtract, op1=ALU.mult), reads=[zb, stb], writes=[zb])
        S.op("dve", lambda e: e.tensor_tensor(out=z[:], in0=z[:], in1=gt[:], op=ALU.mult), reads=[zb, gbb], writes=[zb])
        S.op("dve", lambda e: e.tensor_tensor(out=z[:], in0=z[:], in1=bt[:], op=ALU.add), reads=[zb, gbb], writes=[zb])
        S.dma("sp", lambda e: e.dma_start(out=dst_ap, in_=z[:]), reads=[zb])

    def load_ln(self, pes, g_ap, b_ap):
        gt = self.sbuf(pes, "ln_g", [128, D], F32)
        bt = self.sbuf(pes, "ln_b", [128, D], F32)
        gbb = Buf()
        self.S.dma("sp", lambda e: e.dma_start(out=gt[:], in_=g_ap.partition_broadcast(128)), writes=[gbb])
        self.S.dma("sp", lambda e: e.dma_start(out=bt[:], in_=b_ap.partition_broadcast(128)), writes=[gbb])
        stats = self.sbuf(pes, "ln_stats", [128, 4, 6], F32)
        mv = self.sbuf(pes, "ln_mv", [128, 2], F32)
        sd = self.sbuf(pes, "ln_sd", [128, 1], F32)
        return gt, bt, gbb, (stats, mv, sd), Buf()

    def phase_mem(self):
        S, T = self.S, self.T
        with ExitStack() as pes:
            memT = self.sbuf(pes, "memT", [128, 16, NMEM], BF16)
            memTb = bufs(2)
            wkv = self.sbuf(pes, "wkv", [128, 16, 1024], BF16)
            wkvb = Buf()
            src = T["mem_w_kv"].rearrange("(dc p) n -> p dc n", p=128)
            for i in range(2):
                self.load_w(wkv[:, :, i * 512:(i + 1) * 512], wkvb, src[:, :, i * 512:(i + 1) * 512])
            xr = [self.sbuf(pes, "mx%d" % i, [128, D], F32) for i in range(2)]
            xbr = [self.sbuf(pes, "mxb%d" % i, [128, D], BF16) for i in range(2)]
            for mc in range(2):
                xb_, xbb = Buf(), Buf()
                S.dma("sp", lambda e: e.dma_start(out=xr[mc][:], in_=T["mem"][mc * 128:(mc + 1) * 128, :]), writes=[xb_])
                S.op("act", lambda e: e.activation(out=xbr[mc][:], in_=xr[mc][:], func=AF.Copy), reads=[xb_], writes=[xbb])
                self.transpose_tile(xbr[mc], xbb, memT[:, :, mc * 128:(mc + 1) * 128], [memTb[mc]], mc)
            S.op("pool", lambda e: e.memset(self.memV[:], 1.0), writes=[self.memV_b])
            for hh in range(4):
                k = hh % 2
                for dc in range(16):
                    S.op("pe", lambda e: e.matmul(self.ps[k][:, 0:NMEM], lhsT=wkv[:, dc, hh * 128:(hh + 1) * 128],
                                                  rhs=memT[:, dc, :], start=dc == 0, stop=dc == 15),
                         reads=[wkvb] + memTb, writes=[self.psb[k]])
                S.op("act", lambda e: e.activation(out=self.memKT[:, hh, :], in_=self.ps[k][:, 0:NMEM], func=AF.Copy),
                     reads=[self.psb[k]], writes=[self.memKT_b])
            for mc in range(2):
                k = 2 + mc
                for dc in range(16):
                    S.op("pe", lambda e: e.matmul(self.ps[k][:, :], lhsT=memT[:, dc, mc * 128:(mc + 1) * 128],
                                                  rhs=wkv[:, dc, 512:1024], start=dc == 0, stop=dc == 15),
                         reads=[wkvb] + memTb, writes=[self.psb[k]])
                S.op("act", lambda e: e.activation(out=self.memV[:, mc, :, 0:128],
                                                   in_=self.ps[k][:].rearrange("p (h d) -> p h d", h=4), func=AF.Copy),
                     reads=[self.psb[k]], writes=[self.memV_b])
            S.barrier()

    def attn_core(self, QT, QTb, KT, KTb, krows, n_sc, v_ap, v_bufs, scale, consume, ring_p, mode=None):
        S = self.S
        k0, k1 = krows
        for jg in range(4):
            abank = [2, 3, 4, 5]
            if mode is None:
                scs = list(range(n_sc))
            elif mode["dir"] == "f":
                scs = list(range(0, 4 * jg + 4))
            else:
                scs = list(range(4 * jg, 16))
            pbufs = {}

            def emit_st(si, jg=jg, scs=scs):
                sc = scs[si]
                kst = (jg * 16 + si) % 2
                c0, c1 = 0, 512
                if mode is not None:
                    if mode["dir"] == "f" and sc >= 4 * jg:
                        c0 = (sc - 4 * jg) * 128
                    if mode["dir"] == "b" and sc <= 4 * jg + 3:
                        c1 = (sc - 4 * jg + 1) * 128
                S.op("pe", lambda e: e.matmul(self.ps[kst][:, c0:c1], lhsT=KT[k0:k1, sc * 128:(sc + 1) * 128],
                                              rhs=QT[k0:k1, jg * 512 + c0:jg * 512 + c1], start=True, stop=True),
                     reads=[QTb, KTb], writes=[self.psb[kst]])
                self.bg_tick()
                P, Pb = ring_p.next()
                if mode is None:
                    S.op("act", lambda e: e.activation(out=P[:, c0:c1], in_=self.ps[kst][:, c0:c1], func=AF.Exp, scale=scale),
                         reads=[self.psb[kst]], writes=[Pb])
                else:
                    Wt, Wtb = mode["wring"].next()
                    S.op("act", lambda e: e.activation(out=Wt[:, c0:c1], in_=mode["Rb"][:, jg * 512 + c0:jg * 512 + c1],
                                                       func=AF.Exp, bias=mode["bias"][:, sc, mode["col"]:mode["col"] + 1], scale=1.0),
                         reads=[mode["Rbb"], mode["biasb"]], writes=[Wtb])
                    S.op("dve", lambda e: e.tensor_tensor(out=P[:, c0:c1], in0=self.ps[kst][:, c0:c1], in1=Wt[:, c0:c1], op=ALU.mult),
                         reads=[self.psb[kst], Wtb], writes=[Pb])
                    if 4 * jg <= sc <= 4 * jg + 3:
                        d0 = (sc - 4 * jg) * 128
                        S.op("pool", lambda e: e.tensor_tensor(out=P[:, d0:d0 + 128], in0=P[:, d0:d0 + 128], in1=mode["tri"][:],
                                                               op=ALU.mult), reads=[Pb, mode["trib"]], writes=[Pb])
                pbufs[si] = (P, Pb)

            def emit_pv(si, jg=jg, scs=scs):
                sc = scs[si]
                P, Pb = pbufs.pop(si)
                for j4 in range(4):
                    jc = 4 * jg + j4
                    if mode is None:
                        first, last = 0, n_sc - 1
                    elif mode["dir"] == "f":
                        first, last = 0, jc
                    else:
                        first, last = jc, 15
                    if sc < first or sc > last:
                        continue
                    ab = abank[j4]
                    acc = self.ps[ab][:, 0:130]
                    S.op("pe", lambda e: e.matmul(acc, lhsT=P[:, j4 * 128:(j4 + 1) * 128], rhs=v_ap(sc),
                                                  start=sc == first, stop=sc == last),
                         reads=[Pb] + v_bufs, writes=[self.psb[ab]])

            emit_st(0)
            for si in range(len(scs)):
                if si + 1 < len(scs):
                    emit_st(si + 1)
                emit_pv(si)
            for j4 in range(4):
                ab = abank[j4]
                consume(4 * jg + j4, self.ps[ab][:, 0:130], self.psb[ab])

    def proj_fm(self, w, wb_, hT, hTb, jg, k):
        S = self.S
        for dc in range(16):
            S.op("pe", lambda e: e.matmul(self.ps[k][:, :], lhsT=w[:, dc, :], rhs=hT[:, dc, jg * 512:(jg + 1) * 512],
                                          start=dc == 0, stop=dc == 15),
                 reads=[wb_] + hTb[4 * jg:4 * jg + 4], writes=[self.psb[k]])

    def proj_tm(self, w, wb_, hT, hTb, evac):
        S = self.S
        for g in range(4):
            k = g % 2
            for i in range(4):
                tt = 4 * g + i
                for dc in range(16):
                    S.op("pe", lambda e: e.matmul(self.ps[k][:, i * 128:(i + 1) * 128], lhsT=hT[:, dc, tt * 128:(tt + 1) * 128],
                                                  rhs=w[:, dc, :], start=dc == 0, stop=dc == 15),
                         reads=[wb_, hTb[tt]], writes=[self.psb[k]])
            evac(g, self.ps[k], self.psb[k])

    def mem_heads(self, pes, w_src, col0, hT, hTb, mix_d, wts, ring_p, QT, QTb, mixh, mixhb):
        S = self.S
        sm = self.sbuf(pes, "mh_rs", [128, 1], F32)
        smb = Buf()
        for hh in range(4):
            w, wb_ = wts.next()
            self.load_w(w[:], wb_, w_src[:, :, col0 + hh * 128:col0 + (hh + 1) * 128])
            for jg in range(4):
                self.proj_fm(w, wb_, hT, hTb, jg, jg % 2)
                S.op("act", lambda e: e.activation(out=QT[:, jg * 512:(jg + 1) * 512], in_=self.ps[jg % 2][:, :], func=AF.Copy),
                     reads=[self.psb[jg % 2]], writes=[QTb])

            def consume(jc, acc, accb):
                S.op("dve", lambda e: e.reciprocal(out=sm[:], in_=acc[:, 128:129]), reads=[accb], writes=[smb])
                S.op("dve", lambda e: e.tensor_scalar(out=mixh[:, jc, :], in0=acc[:, 0:128], scalar1=sm[:, 0:1], scalar2=None,
                                                      op0=ALU.mult), reads=[accb, smb], writes=[mixhb])

            self.attn_core(QT, QTb, self.memKT[:, hh, :], self.memKT_b, (0, 128), 2,
                           lambda sc: self.memV[:, sc, hh, 0:130], [self.memV_b], 128 ** -0.5, consume, ring_p)
            c = 1536 + hh * 128
            S.dma("sp", lambda e: e.dma_start(out=mix_d.rearrange("(jc p) f -> p jc f", p=128)[:, :, c:c + 128], in_=mixh[:]),
                  reads=[mixhb])

    def phase_attn(self, h_d, mix_d, bg_tables=None):
        S, T = self.S, self.T
        lam_init = 0.8 - 0.6 * math.exp(-0.3 * 0)
        with ExitStack() as pes:
            hT = self.sbuf(pes, "hT", [128, 16, SEQ], BF16)
            hTb = bufs(NT)
            with ExitStack() as les:
                self.load_T(les, h_d, hT, hTb)
                S.barrier()
            if bg_tables:
                self.bg_start(pes, bg_tables, 12 * 2 * 64 + 4 * 8)
            sb = lambda n, s, d: self.sbuf(pes, n, s, d)
            CS = sb("CS", [128, SEQ], F32)
            SN = sb("SN", [128, SEQ], F32)
            rb = Buf()
            with ExitStack() as res:
                posi = self.sbuf(res, "posi", [128, SEQ], I32)
                y = self.sbuf(res, "ry", [128, SEQ], F32)
                kf = self.sbuf(res, "rkf", [128, SEQ], F32)
                ki = self.sbuf(res, "rki", [128, SEQ], I32)
                cst = self.sbuf(res, "rcst", [128, 2], F32)
                tb = Buf()
                S.dma("sp", lambda e: e.dma_start(out=posi[:], in_=T["pos"].partition_broadcast(128)), writes=[tb])
                S.dma("sp", lambda e: e.dma_start(out=cst[:], in_=T["c_rope"][:, :]), writes=[tb])
                S.op("dve", lambda e: e.tensor_copy(out=kf[:], in_=posi[:]), reads=[tb], writes=[tb])
                for which, dst in ((0, SN), (1, CS)):
                    S.op("dve", lambda e: e.tensor_scalar(out=y[:], in0=kf[:], scalar1=cst[:, 0:1], scalar2=0.25 * which,
                                                          op0=ALU.mult, op1=ALU.add), reads=[tb], writes=[rb])
                    S.op("dve", lambda e: e.tensor_copy(out=ki[:], in_=y[:]), reads=[rb], writes=[rb])
                    S.op("dve", lambda e: e.tensor_copy(out=dst[:], in_=ki[:]), reads=[rb], writes=[rb])
                    S.op("dve", lambda e: e.tensor_tensor(out=y[:], in0=y[:], in1=dst[:], op=ALU.subtract), reads=[rb], writes=[rb])
                    S.op("dve", lambda e: e.tensor_single_scalar(out=dst[:], in_=y[:], scalar=0.5, op=ALU.is_gt), reads=[rb], writes=[rb])
                    S.op("dve", lambda e: e.tensor_tensor(out=y[:], in0=y[:], in1=dst[:], op=ALU.subtract), reads=[rb], writes=[rb])
                    S.op("dve", lambda e: e.tensor_single_scalar(out=dst[:], in_=y[:], scalar=-0.5, op=ALU.is_lt), reads=[rb], writes=[rb])
                    S.op("dve", lambda e: e.tensor_tensor(out=y[:], in0=y[:], in1=dst[:], op=ALU.add), reads=[rb], writes=[rb])
                    S.op("act", lambda e: e.activation(out=dst[:], in_=y[:], func=AF.Sin, scale=2.0 * math.pi), reads=[rb], writes=[rb])
                S.op("dve", lambda e: e.tensor_scalar(out=SN[:], in0=SN[:], scalar1=cst[:, 1:2], scalar2=None, op0=ALU.mult),
                     reads=[rb, tb], writes=[rb])
                S.barrier()
            lamt = sb("lamt", [128, 4, 64], F32)
            lw = sb("lamw", [128, 8], F32)
            lb = Buf()
            S.dma("sp", lambda e: e.dma_start(out=lamt[:].rearrange("p a b -> p (a b)"),
                                              in_=T["attn_lambda"].partition_broadcast(128)), writes=[lb])
            lj = sb("lamj", [128, 64], F32)
            for i in range(2):
                S.op("dve", lambda e: e.scalar_tensor_tensor(out=lj[:], in0=lamt[:, 2 * i, :], scalar=1.0, in1=lamt[:, 2 * i + 1, :],
                                                             op0=ALU.mult, op1=ALU.mult, accum_out=lw[:, i:i + 1]),
                     reads=[lb], writes=[lb])
            S.op("act", lambda e: e.activation(out=lw[:, 2:4], in_=lw[:, 0:2], func=AF.Exp), reads=[lb], writes=[lb])
            S.op("dve", lambda e: e.tensor_tensor(out=lw[:, 4:5], in0=lw[:, 3:4], in1=lw[:, 2:3], op=ALU.subtract), reads=[lb], writes=[lb])
            S.op("dve", lambda e: e.tensor_single_scalar(out=lw[:, 5:6], in_=lw[:, 4:5], scalar=-lam_init, op=ALU.add), reads=[lb], writes=[lb])
            neglam = lw[:, 5:6]
            G = sb("Gn", [128, 1536], F32)
            Gb = Buf()
            S.dma("sp", lambda e: e.dma_start(out=G[:], in_=T["attn_head_norm"].partition_broadcast(128)), writes=[Gb])
            wts = Ring([sb("aw%d" % i, [128, 16, 128], BF16) for i in range(6)])
            QT = sb("QT", [128, SEQ], BF16)
            KT = sb("KT", [128, SEQ], BF16)
            QTb, KTb = Buf(), Buf()
            V = sb("Vaug", [128, 16, 144], BF16)
            Vb = Buf()
            S.op("pool", lambda e: e.memset(V[:], 1.0), writes=[Vb])
            ring_p = Ring([sb("P%d" % i, [128, 512], BF16) for i in range(3)])
            t1r = Ring([sb("rt1%d" % i, [128, 512], F32) for i in range(2)])
            t2r = Ring([sb("rt2%d" % i, [128, 512], F32) for i in range(2)])
            om = [sb("om%d" % i, [128, 16, 128], F32) for i in range(2)]
            omb = bufs(2)
            sm = sb("a_rs", [128, 1], F32)
            smb = Buf()
            sq = sb("a_sq", [128, 16, 128], F32)
            ms = sb("a_ms", [128, 16], F32)
            mixh = sb("mixh", [128, 16, 128], BF16)
            mixhb = Buf()
            W = T["attn_w_in"].rearrange("(dc p) n -> p dc n", p=128)
            WR = T["attn_w_rot"].rearrange("(dc p) n -> p dc n", p=128)
            for h in range(12):
                for qi, (dst, dstb) in enumerate(((QT, QTb), (KT, KTb))):
                    w, wb_ = wts.next()
                    wr, wrb = wts.next()
                    c = qi * 1536 + h * 128
                    self.load_w(w[:], wb_, W[:, :, c:c + 128])
                    self.load_w(wr[:], wrb, WR[:, :, c:c + 128])
                    for jg in range(4):
                        self.proj_fm(w, wb_, hT, hTb, jg, 0)
                        self.proj_fm(wr, wrb, hT, hTb, jg, 1)
                        t1, t1b = t1r.next()
                        t2, t2b = t2r.next()
                        js = slice(jg * 512, (jg + 1) * 512)
                        S.op("dve", lambda e: e.tensor_tensor(out=t1[:], in0=self.ps[0][:, :], in1=CS[:, js], op=ALU.mult),
                             reads=[self.psb[0], rb], writes=[t1b])
                        S.op("dve", lambda e: e.tensor_tensor(out=t2[:], in0=self.ps[1][:, :], in1=SN[:, js], op=ALU.mult),
                             reads=[self.psb[1], rb], writes=[t2b])
                        S.op("pool", lambda e: e.tensor_tensor(out=dst[:, js], in0=t1[:], in1=t2[:], op=ALU.add),
                             reads=[t1b, t2b], writes=[dstb])
                w, wb_ = wts.next()
                c = 2 * 1536 + h * 128
                self.load_w(w[:], wb_, W[:, :, c:c + 128])

                def evac_v(g, pst, pstb):
                    S.op("act", lambda e: e.activation(out=V[:, 4 * g:4 * g + 4, 0:128],
                                                       in_=pst[:].rearrange("p (i d) -> p i d", i=4), func=AF.Copy),
                         reads=[pstb], writes=[Vb])

                self.proj_tm(w, wb_, hT, hTb, evac_v)
                for m in range(2):
                    def consume(jc, acc, accb, m=m):
                        S.op("dve", lambda e: e.reciprocal(out=sm[:], in_=acc[:, 128:129]), reads=[accb], writes=[smb])
                        S.op("dve", lambda e: e.tensor_scalar(out=om[m][:, jc, :], in0=acc[:, 0:128], scalar1=sm[:, 0:1], scalar2=None,
                                                              op0=ALU.mult), reads=[accb, smb], writes=[omb[m]])

                    self.attn_core(QT, QTb, KT, KTb, (m * 64, (m + 1) * 64), 16, lambda sc: V[:, sc, 0:130], [Vb], 64 ** -0.5,
                                   consume, ring_p)
                o0 = om[0][:].rearrange("p a b -> p (a b)")
                o1 = om[1][:].rearrange("p a b -> p (a b)")
                S.op("dve", lambda e: e.scalar_tensor_tensor(out=o0, in0=o1, scalar=neglam, in1=o0, op0=ALU.mult, op1=ALU.add),
                     reads=[omb[0], omb[1], lb], writes=[omb[0]])
                self.head_norm_out(om[0], omb[0], sq, ms, G[:, h * 128:(h + 1) * 128], Gb, 1.0 - lam_init, None, mixh, mixhb, smb)
                S.dma("sp", lambda e: e.dma_start(out=mix_d.rearrange("(jc p) f -> p jc f", p=128)[:, :, h * 128:(h + 1) * 128],
                                                  in_=mixh[:]), reads=[mixhb])
            self.mem_heads(pes, W, 3 * 1536, hT, hTb, mix_d, wts, ring_p, QT, QTb, mixh, mixhb)
            self.bg_finish()
            S.barrier()

    def head_norm_out(self, a, ab, sq, ms, g_ap, gb, mult, og, mixh, mixhb, smb):
        S = self.S
        S.op("pool", lambda e: e.tensor_tensor(out=sq[:], in0=a[:], in1=a[:], op=ALU.mult), reads=[ab], writes=[smb])
        S.op("dve", lambda e: e.tensor_reduce(out=ms[:], in_=sq[:], axis=AX.X, op=ALU.add), reads=[smb], writes=[smb])
        S.op("act", lambda e: e.activation(out=ms[:], in_=ms[:], func=AF.Sqrt, bias=self.eps_t[:], scale=1.0 / 128.0),
             reads=[smb, self.eps_b], writes=[smb])
        S.op("dve", lambda e: e.reciprocal(out=ms[:], in_=ms[:]), reads=[smb], writes=[smb])
        S.op("dve", lambda e: e.tensor_tensor(out=a[:], in0=a[:], in1=ms[:].unsqueeze(2).to_broadcast([128, 16, 128]), op=ALU.mult),
             reads=[ab, smb], writes=[ab])
        if og is not None:
            S.op("pool", lambda e: e.tensor_tensor(out=a[:], in0=a[:], in1=og[0][:], op=ALU.mult), reads=[ab, og[1]], writes=[ab])
        S.op("dve", lambda e: e.scalar_tensor_tensor(out=mixh[:], in0=a[:], scalar=float(mult),
                                                     in1=g_ap.unsqueeze(1).to_broadcast([128, 16, 128]),
                                                     op0=ALU.mult, op1=ALU.mult), reads=[ab, gb], writes=[mixhb])

    def phase_mlstm(self, h_d, mix_d, bg_tables=None):
        S, T = self.S, self.T
        with ExitStack() as pes:
            hT = self.sbuf(pes, "hT", [128, 16, SEQ], BF16)
            hTb = bufs(NT)
            with ExitStack() as les:
                self.load_T(les, h_d, hT, hTb)
                S.barrier()
            if bg_tables:
                self.bg_start(pes, bg_tables, 12 * 80 + 4 * 8)
            sb = lambda n, s, d: self.sbuf(pes, n, s, d)
            W = T["mlstm_w_in"].rearrange("(dc p) n -> p dc n", p=128)
            XG = sb("XG", [48, SEQ], F32)
            CUM = sb("CUM", [48, SEQ], F32)
            SUF = sb("SUF", [48, SEQ], F32)
            gb_ = Buf()
            BIAS = sb("BIAS", [128, 16, 24], F32)
            biasb = Buf()
            with ExitStack() as ges:
                LF = self.sbuf(ges, "LF", [48, SEQ], F32)
                wg = self.sbuf(ges, "wg", [128, 16, 48], BF16)
                wgb = Buf()
                self.load_w(wg[:], wgb, W[:, :, 6656:6704])
                gbias = self.sbuf(ges, "gbias", [48, 1], F32)
                ones = self.sbuf(ges, "gones", [48, SEQ], F32)
                dsel = self.sbuf(ges, "dsel", [48, 3, 24], F32)
                cb = Buf()
                S.dma("sp", lambda e: e.dma_start(out=gbias[:], in_=T["gate_bias"][:, :]), writes=[cb])
                S.dma("sp", lambda e: e.dma_start(out=dsel[:], in_=T["c_dsel"][:, :, :]), writes=[cb])
                S.op("pool", lambda e: e.memset(ones[:], 1.0), writes=[cb])
                for jg in range(4):
                    js = slice(jg * 512, (jg + 1) * 512)
                    for dc in range(16):
                        S.op("pe", lambda e: e.matmul(self.ps[jg % 2][0:48, :], lhsT=wg[:, dc, :], rhs=hT[:, dc, js],
                                                      start=dc == 0, stop=dc == 15),
                             reads=[wgb] + hTb[4 * jg:4 * jg + 4], writes=[self.psb[jg % 2]])
                    S.op("dve", lambda e: e.tensor_scalar(out=XG[:, js], in0=self.ps[jg % 2][0:48, :], scalar1=gbias[:, 0:1],
                                                          scalar2=None, op0=ALU.add), reads=[self.psb[jg % 2], cb], writes=[gb_])
                S.op("act", lambda e: e.activation(out=LF[:], in_=XG[:], func=AF.Exp, scale=-1.0), reads=[gb_], writes=[gb_])
                S.op("act", lambda e: e.activation(out=LF[:], in_=LF[:], func=AF.Ln, bias=1.0, scale=1.0), reads=[gb_], writes=[gb_])
                S.op("dve", lambda e: e.tensor_single_scalar(out=LF[:], in_=LF[:], scalar=-1.0, op=ALU.mult), reads=[gb_], writes=[gb_])
                S.op("dve", lambda e: e.tensor_tensor_scan(out=CUM[:], data0=ones[:], data1=LF[:], initial=0.0,
                                                           op0=ALU.mult, op1=ALU.add), reads=[gb_, cb], writes=[gb_])
                S.op("dve", lambda e: e.tensor_tensor(out=SUF[:], in0=LF[:], in1=CUM[:], op=ALU.subtract), reads=[gb_], writes=[gb_])
                S.op("dve", lambda e: e.tensor_scalar(out=SUF[:], in0=SUF[:], scalar1=CUM[:, SEQ - 1:SEQ], scalar2=None, op0=ALU.add),
                     reads=[gb_], writes=[gb_])
                for sc in range(16):
                    ss = slice(sc * 128, (sc + 1) * 128)
                    o = self.ps[2][:, sc * 24:(sc + 1) * 24]
                    for i, src in enumerate((XG, CUM, SUF)):
                        S.op("pe", lambda e: e.matmul(o, lhsT=src[:, ss], rhs=dsel[:, i, :], start=i == 0, stop=i == 2),
                             reads=[gb_, cb], writes=[self.psb[2]])
                S.op("dve", lambda e: e.tensor_copy(out=BIAS[:].rearrange("p a b -> p (a b)"), in_=self.ps[2][:, 0:384]),
                     reads=[self.psb[2]], writes=[biasb])
                S.barrier()
            selr = sb("selr", [48, 24, 128], F32)
            tri = sb("tri", [128, 2, 128], F32)
            cb2 = Buf()
            S.dma("sp", lambda e: e.dma_start(out=selr[:], in_=T["c_selr"][:, :, :]), writes=[cb2])
            S.dma("sp", lambda e: e.dma_start(out=tri[:], in_=T["c_tri"][:, :, :]), writes=[cb2])
            G = sb("Gn", [128, 1536], F32)
            Gb = Buf()
            S.dma("sp", lambda e: e.dma_start(out=G[:], in_=T["mlstm_head_norm"].partition_broadcast(128)), writes=[Gb])
            wts = Ring([sb("aw%d" % i, [128, 16, 128], BF16) for i in range(6)])
            QT = sb("QT", [128, SEQ], BF16)
            KT = sb("KT", [128, SEQ], BF16)
            QTb, KTb = Buf(), Buf()
            V = sb("Vaug", [128, 16, 144], BF16)
            Vb = Buf()
            S.op("pool", lambda e: e.memset(V[:], 1.0), writes=[Vb])
            OG = sb("OG", [128, 16, 128], BF16)
            OGb = Buf()
            ring_p = Ring([sb("P%d" % i, [128, 512], BF16) for i in range(3)])
            wring = Ring([sb("Wt%d" % i, [128, 512], F32) for i in range(3)])
            Rb = sb("Rb", [128, SEQ], F32)
            Rbb = Buf()
            HS = sb("HS", [128, 16, 128], F32)
            HSb = Buf()
            HD = sb("HD", [128, 128], F32)
            dn = sb("m_dn", [128, 1], F32)
            dn0 = sb("m_dn0", [128, 1], F32)
            smb = Buf()
            sq = sb("a_sq", [128, 16, 128], F32)
            ms = sb("a_ms", [128, 16], F32)
            mixh = sb("mixh", [128, 16, 128], BF16)
            mixhb = Buf()
            for h in range(12):
                for qi, (dst, dstb, scl) in enumerate(((QT, QTb, 1.0), (KT, KTb, 128 ** -0.5))):
                    w, wb_ = wts.next()
                    c = qi * 1536 + h * 128
                    self.load_w(w[:], wb_, W[:, :, c:c + 128])
                    for jg in range(4):
                        self.proj_fm(w, wb_, hT, hTb, jg, jg % 2)
                        S.op("act", lambda e: e.activation(out=dst[:, jg * 512:(jg + 1) * 512], in_=self.ps[jg % 2][:, :],
                                                           func=AF.Copy, scale=scl), reads=[self.psb[jg % 2]], writes=[dstb])
                w, wb_ = wts.next()
                self.load_w(w[:], wb_, W[:, :, 2 * 1536 + h * 128:2 * 1536 + (h + 1) * 128])

                def evac_v(g, pst, pstb):
                    S.op("act", lambda e: e.activation(out=V[:, 4 * g:4 * g + 4, 0:128],
                                                       in_=pst[:].rearrange("p (i d) -> p i d", i=4), func=AF.Copy),
                         reads=[pstb], writes=[Vb])

                self.proj_tm(w, wb_, hT, hTb, evac_v)
                w, wb_ = wts.next()
                self.load_w(w[:], wb_, W[:, :, 3 * 1536 + h * 128:3 * 1536 + (h + 1) * 128])

                def evac_o(g, pst, pstb):
                    S.op("act", lambda e: e.activation(out=OG[:, 4 * g:4 * g + 4, :],
                                                       in_=pst[:].rearrange("p (i d) -> p i d", i=4), func=AF.Sigmoid),
                         reads=[pstb], writes=[OGb])

                self.proj_tm(w, wb_, hT, hTb, evac_o)
                for di, dr in enumerate(("f", "b")):
                    src = CUM if dr == "f" else SUF
                    col = h if dr == "f" else 12 + h
                    for jg in range(4):
                        k = 6 + jg % 2
                        S.op("pe", lambda e: e.matmul(self.ps[k][:, :], lhsT=selr[:, col, :], rhs=src[:, jg * 512:(jg + 1) * 512],
                                                      start=True, stop=True), reads=[gb_, cb2], writes=[self.psb[k]])
                        S.op("dve", lambda e: e.tensor_copy(out=Rb[:, jg * 512:(jg + 1) * 512], in_=self.ps[k][:, :]),
                             reads=[self.psb[k]], writes=[Rbb])

                    def consume(jc, acc, accb, di=di):
                        S.op("dve", lambda e: e.tensor_copy(out=dn0[:], in_=acc[:, 128:129]), reads=[accb], writes=[smb])
                        S.op("dve", lambda e: e.scalar_tensor_tensor(out=dn[:], in0=dn0[:], scalar=-1.0, in1=dn0[:], op0=ALU.mult,
                                                                     op1=ALU.max), reads=[smb], writes=[smb])
                        S.op("dve", lambda e: e.tensor_scalar(out=dn[:], in0=dn[:], scalar1=1.0, scalar2=None, op0=ALU.max),
                             reads=[smb], writes=[smb])
                        S.op("dve", lambda e: e.reciprocal(out=dn[:], in_=dn[:]), reads=[smb], writes=[smb])
                        if di == 0:
                            S.op("dve", lambda e: e.tensor_scalar(out=HS[:, jc, :], in0=acc[:, 0:128], scalar1=dn[:, 0:1], scalar2=None,
                                                                  op0=ALU.mult), reads=[accb, smb], writes=[HSb])
                        else:
                            S.op("dve", lambda e: e.scalar_tensor_tensor(out=HS[:, jc, :], in0=acc[:, 0:128], scalar=dn[:, 0:1],
                                                                         in1=HS[:, jc, :], op0=ALU.mult, op1=ALU.add),
                                 reads=[accb, smb, HSb], writes=[HSb])

                    mode = dict(dir=dr, wring=wring, Rb=Rb, Rbb=Rbb, bias=BIAS, biasb=biasb, col=col, tri=tri[:, di, :], trib=cb2)
                    self.attn_core(QT, QTb, KT, KTb, (0, 128), 16, lambda sc: V[:, sc, 0:130], [Vb], 1.0, consume, ring_p, mode=mode)
                self.head_norm_out(HS, HSb, sq, ms, G[:, h * 128:(h + 1) * 128], Gb, 1.0, (OG, OGb), mixh, mixhb, smb)
                S.dma("sp", lambda e: e.dma_start(out=mix_d.rearrange("(jc p) f -> p jc f", p=128)[:, :, h * 128:(h + 1) * 128],
                                                  in_=mixh[:]), reads=[mixhb])
            self.mem_heads(pes, W, 4 * 1536, hT, hTb, mix_d, wts, ring_p, QT, QTb, mixh, mixhb)
            self.bg_finish()
            S.barrier()

    def phase_linear(self, mix_d, w_ap, h_d, g_ap, b_ap, out_d):
        S, T = self.S, self.T
        with ExitStack() as pes:
            sb = lambda n, s, d: self.sbuf(pes, n, s, d)
            Wb = sb("Wout", [128, 16, D], BF16)
            Wbb = Buf()
            src = w_ap.rearrange("(dc p) n -> p dc n", p=128)
            for i in range(4):
                self.load_w(Wb[:, :, i * 512:(i + 1) * 512], Wbb, src[:, :, i * 512:(i + 1) * 512])
            gt, bt, gbb, st, stb = self.load_ln(pes, g_ap, b_ap)
            mx = Ring([sb("lmx%d" % i, [128, D], BF16) for i in range(2)])
            mT = Ring([sb("lmT%d" % i, [128, 16, 128], BF16) for i in range(2)])
            hx = Ring([sb("lhx%d" % i, [128, D], F32) for i in range(2)])
            zr = Ring([sb("lz%d" % i, [128, D], F32) for i in range(2)])
            for tt in range(NT):
                ts = slice(tt * 128, (tt + 1) * 128)
                m, mb = mx.next()
                S.dma("sp", lambda e: e.dma_start(out=m[:], in_=mix_d[ts, :]), writes=[mb])
                x, xb_ = hx.next()
                S.dma("sp", lambda e: e.dma_start(out=x[:], in_=h_d[ts, :]), writes=[xb_])
                t, tb = mT.next()
                self.transpose_tile(m, mb, t[:], [tb], tt)
                z, zb = zr.next()
                for nb in range(4):
                    k = [0, 1, 2, 3][nb] if tt % 2 == 0 else [4, 5, 2, 3][nb]
                    for c in range(16):
                        S.op("pe", lambda e: e.matmul(self.ps[k][:, :], lhsT=t[:, c, :], rhs=Wb[:, c, nb * 512:(nb + 1) * 512],
                                                      start=c == 0, stop=c == 15), reads=[tb, Wbb], writes=[self.psb[k]])
                    S.op("dve", lambda e: e.scalar_tensor_tensor(out=z[:, nb * 512:(nb + 1) * 512], in0=x[:, nb * 512:(nb + 1) * 512],
                                                                 scalar=ALPHA, in1=self.ps[k][:, :], op0=ALU.mult, op1=ALU.add),
                         reads=[xb_, self.psb[k]], writes=[zb])
                self.ln_store(z, zb, gt, bt, gbb, out_d[ts, :], st, stb)
            S.barrier()

    def conv_jobs(self, es, tables):
        S = self.S
        fr = Ring([self.sbuf(es, "bgf%d" % i, [128, 1024], F32) for i in range(2)])
        br = Ring([self.sbuf(es, "bgb%d" % i, [128, 1024], BF16) for i in range(2)])
        jobs = []
        for src, dst in tables:
            sv = src.rearrange("(p r) d -> p r d", p=128)
            dv = dst.rearrange("(p r) d -> p r d", p=128)
            for r in range(128):
                for hf in range(2):
                    jobs.append((sv[:, r, hf * 1024:(hf + 1) * 1024], dv[:, r, hf * 1024:(hf + 1) * 1024]))
        self.bg_total = len(jobs)
        loaded = []

        def load(i):
            f, fb = fr.next()
            S.dma("sp", lambda e: e.dma_start(out=f[:], in_=jobs[i][0]), writes=[fb])
            loaded.append((f, fb))

        load(0)
        for i in range(len(jobs)):
            if i + 1 < len(jobs):
                load(i + 1)
            f, fb = loaded[i]
            b, bb = br.next()
            S.op("pool", lambda e: e.tensor_copy(out=b[:], in_=f[:]), reads=[fb], writes=[bb])
            S.dma("sp", lambda e: e.dma_start(out=jobs[i][1], in_=b[:]), reads=[bb])
            yield

    def bg_start(self, es, tables, nsteps):
        self.bg = self.conv_jobs(es, tables)
        self.bg_acc = 0.0
        self.bg_rate = (len(tables) * 256) / float(nsteps)

    def bg_tick(self):
        if getattr(self, "bg", None) is None:
            return
        self.bg_acc += self.bg_rate
        while self.bg_acc >= 1.0:
            self.bg_acc -= 1.0
            if next(self.bg, "done") == "done":
                self.bg = None
                return

    def bg_finish(self):
        if getattr(self, "bg", None) is not None:
            for _ in self.bg:
                pass
            self.bg = None

    def phase_convert(self, tables):
        S = self.S
        with ExitStack() as pes:
            fr = Ring([self.sbuf(pes, "cvf%d" % i, [128, 4096], F32) for i in range(3)])
            br = Ring([self.sbuf(pes, "cvb%d" % i, [128, 4096], BF16) for i in range(3)])
            jobs = []
            for src, dst in tables:
                sv = src.rearrange("(p r) d -> p r d", p=128)
                dv = dst.rearrange("(p r) d -> p r d", p=128)
                for c in range(64):
                    jobs.append((sv[:, 2 * c:2 * c + 2, :], dv[:, 2 * c:2 * c + 2, :]))
            loaded = []

            def load(i):
                f, fb = fr.next()
                S.dma("sp", lambda e: e.dma_start(out=f[:].rearrange("p (r d) -> p r d", r=2), in_=jobs[i][0]), writes=[fb])
                loaded.append((f, fb))

            for i in range(min(2, len(jobs))):
                load(i)
            for i in range(len(jobs)):
                if i + 2 < len(jobs):
                    load(i + 2)
                f, fb = loaded[i]
                b, bb = br.next()
                eng = ("act", "dve", "pool")[i % 3]
                if eng == "act":
                    S.op("act", lambda e: e.activation(out=b[:], in_=f[:], func=AF.Copy), reads=[fb], writes=[bb])
                else:
                    S.op(eng, lambda e: e.tensor_copy(out=b[:], in_=f[:]), reads=[fb], writes=[bb])
                S.dma("sp", lambda e: e.dma_start(out=jobs[i][1], in_=b[:].rearrange("p (r d) -> p r d", r=2)), reads=[bb])
            S.barrier()

    def phase_peer(self, li, h_d, out_d, ub_d, vb_d):
        S, T = self.S, self.T
        with ExitStack() as pes:
            sb = lambda n, s, d: self.sbuf(pes, n, s, d)
            Wq = sb("Wq", [128, 16, D], BF16)
            Wqb = Buf()
            src = T["peer_w_query"][li].rearrange("(dc p) n -> p dc n", p=128)
            for i in range(4):
                self.load_w(Wq[:, :, i * 512:(i + 1) * 512], Wqb, src[:, :, i * 512:(i + 1) * 512])
            keysT = sb("keysT", [128, 16, 128], BF16)
            kb = Buf()
            self.load_w(keysT[:], kb, T["peer_keysT"][li])
            gt, bt, gbb, st, stb = self.load_ln(pes, T["ln_ffn_g"][li:li + 1, :], T["ln_ffn_b"][li:li + 1, :])
            iota = sb("iota", [128, 256], F32)
            ib = Buf()
            S.dma("sp", lambda e: e.dma_start(out=iota[:], in_=T["c_iota"][:, :]), writes=[ib])
            hx = Ring([sb("px%d" % i, [128, D], F32) for i in range(3)])
            xbf_r = Ring([sb("pxb%d" % i, [128, D], BF16) for i in range(2)])
            hTt = sb("phT", [128, 16, 128], BF16)
            hTtb = Buf()
            qT = sb("pqT", [128, 16, 128], BF16)
            qTb = Buf()
            Ssb = sb("pS", [128, 16, 128], F32)
            Sb = Buf()
            tmpa = sb("ptmp", [128, 256], F32)
            v16 = sb("pv16", [128, 2, 16], F32)
            i16u = sb("pi16u", [128, 2, 16], U32)
            i16 = sb("pi16", [128, 2, 16], F32)
            cand = sb("pcand", [128, 16, 16], F32)
            cidx = sb("pcidx", [128, 16, 16], F32)
            c16 = sb("pc16", [128, 16], F32)
            p16u = sb("pp16u", [128, 16], U32)
            p16 = sb("pp16", [128, 16], F32)
            g16 = sb("pg16", [128, 16, 16], F32)
            ab16 = sb("pab16", [128, 2, 16], F32)
            e2 = sb("pe2", [128, 2, 16], F32)
            thr = sb("pthr", [128, 16], F32)
            S.op("dve", lambda e: e.tensor_single_scalar(out=thr[:], in_=iota[:, 1:17], scalar=16.0, op=ALU.mult), reads=[ib], writes=[ib])
            tkb = Buf()
            eidf = sb("peidf", [128, 128], F32)
            gsum = sb("pgsum", [128, 8], F32)
            nmax = sb("pnmax", [128, 8], F32)
            eidx_r = Ring([sb("peidx%d" % i, [128, 128], I32) for i in range(3)])
            gate_r = Ring([sb("pgate%d" % i, [128, 128], F32) for i in range(3)])
            rows = Ring([sb("prow%d" % i, [128, D], BF16) for i in range(9)])
            dgr = Ring([sb("pdg%d" % i, [128, 128], BF16) for i in range(4)])
            junk_r = Ring([sb("pjunk%d" % i, [128, D], BF16) for i in range(3)])
            z = sb("pz", [128, D], F32)
            zb = Buf()

            def gen_a(tt, ctx):
                ts = slice(tt * 128, (tt + 1) * 128)
                x, xb_ = hx.next()
                eidx, eb = eidx_r.next()
                gate, _ = gate_r.next()
                xbf, xbfb = xbf_r.next()
                S.dma("sp", lambda e: e.dma_start(out=x[:], in_=h_d[ts, :]), writes=[xb_])
                S.op("act", lambda e: e.activation(out=xbf[:], in_=x[:], func=AF.Copy), reads=[xb_], writes=[xbfb])
                self.transpose_tile(xbf, xbfb, hTt[:], [hTtb], tt, banks=(2, 3))
                yield
                for half in range(2):
                    for hq in range(8):
                        hp = half * 8 + hq
                        k = hq // 4
                        for dc in range(16):
                            S.op("pe", lambda e: e.matmul(self.ps[k][:, (hq % 4) * 128:(hq % 4 + 1) * 128],
                                                          lhsT=Wq[:, dc, hp * 128:(hp + 1) * 128], rhs=hTt[:, dc, :],
                                                          start=dc == 0, stop=dc == 15), reads=[Wqb, hTtb], writes=[self.psb[k]])
                    for k in range(2):
                        S.op("act", lambda e: e.activation(out=qT[:, half * 8 + 4 * k:half * 8 + 4 * k + 4, :],
                                                           in_=self.ps[k][:].rearrange("p (a b) -> p a b", a=4), func=AF.Copy),
                             reads=[self.psb[k]], writes=[qTb])
                    yield
                for half in range(2):
                    for hq in range(8):
                        hp = half * 8 + hq
                        k = hq // 4
                        S.op("pe", lambda e: e.matmul(self.ps[k][:, (hq % 4) * 128:(hq % 4 + 1) * 128], lhsT=qT[:, hp, :],
                                                      rhs=keysT[:, hp, :], start=True, stop=True), reads=[qTb, kb], writes=[self.psb[k]])
                    for k in range(2):
                        S.op("act", lambda e: e.activation(out=Ssb[:, half * 8 + 4 * k:half * 8 + 4 * k + 4, :],
                                                           in_=self.ps[k][:].rearrange("p (a b) -> p a b", a=4), func=AF.Copy),
                             reads=[self.psb[k]], writes=[Sb])
                    yield
                for h in range(8):
                    for p in range(2):
                        s_ap = Ssb[:, 2 * h + p, :]
                        S.op("dve", lambda e: e.max(out=v16[:, p, 0:8], in_=s_ap), reads=[Sb], writes=[tkb])
                        S.op("dve", lambda e: e.max_index(out=i16u[:, p, 0:8], in_max=v16[:, p, 0:8], in_values=s_ap), reads=[Sb, tkb], writes=[tkb])
                        S.op("dve", lambda e: e.match_replace(out=tmpa[:, 0:128], in_to_replace=v16[:, p, 0:8], in_values=s_ap,
                                                              imm_value=NEG), reads=[Sb, tkb], writes=[tkb])
                        S.op("dve", lambda e: e.max(out=v16[:, p, 8:16], in_=tmpa[:, 0:128]), reads=[tkb], writes=[tkb])
                        S.op("dve", lambda e: e.max_index(out=i16u[:, p, 8:16], in_max=v16[:, p, 8:16], in_values=tmpa[:, 0:128]),
                             reads=[tkb], writes=[tkb])
                    S.op("dve", lambda e: e.tensor_copy(out=i16[:], in_=i16u[:]), reads=[tkb], writes=[tkb])
                    S.op("dve", lambda e: e.tensor_single_scalar(out=i16[:, 0, :], in_=i16[:, 0, :], scalar=128.0, op=ALU.mult),
                         reads=[tkb], writes=[tkb])
                    S.op("dve", lambda e: e.tensor_tensor(out=cand[:], in0=v16[:, 0, :].unsqueeze(2).to_broadcast([128, 16, 16]),
                                                          in1=v16[:, 1, :].unsqueeze(1).to_broadcast([128, 16, 16]), op=ALU.add),
                         reads=[tkb], writes=[tkb])
                    cf = cand[:].rearrange("p a b -> p (a b)")
                    S.op("dve", lambda e: e.max(out=c16[:, 0:8], in_=cf), reads=[tkb], writes=[tkb])
                    S.op("dve", lambda e: e.match_replace(out=tmpa[:], in_to_replace=c16[:, 0:8], in_values=cf, imm_value=NEG),
                         reads=[tkb], writes=[tkb])
                    S.op("dve", lambda e: e.max(out=c16[:, 8:16], in_=tmpa[:]), reads=[tkb], writes=[tkb])
                    S.op("dve", lambda e: e.max_index(out=p16u[:, 0:8], in_max=c16[:, 0:8], in_values=cf), reads=[tkb], writes=[tkb])
                    S.op("dve", lambda e: e.max_index(out=p16u[:, 8:16], in_max=c16[:, 8:16], in_values=tmpa[:]), reads=[tkb], writes=[tkb])
                    S.op("dve", lambda e: e.tensor_copy(out=p16[:], in_=p16u[:]), reads=[tkb], writes=[tkb])
                    S.op("dve", lambda e: e.tensor_tensor(out=g16[:], in0=p16[:].unsqueeze(2).to_broadcast([128, 16, 16]),
                                                          in1=thr[:].unsqueeze(1).to_broadcast([128, 16, 16]), op=ALU.is_ge),
                         reads=[tkb, ib], writes=[tkb])
                    S.op("dve", lambda e: e.tensor_reduce(out=ab16[:, 0, :], in_=g16[:], axis=AX.X, op=ALU.add), reads=[tkb], writes=[tkb])
                    S.op("dve", lambda e: e.scalar_tensor_tensor(out=ab16[:, 1, :], in0=ab16[:, 0, :], scalar=-16.0, in1=p16[:],
                                                                 op0=ALU.mult, op1=ALU.add), reads=[tkb], writes=[tkb])
                    for p in range(2):
                        S.op("dve", lambda e: e.tensor_tensor(out=g16[:], in0=iota[:, 0:16].unsqueeze(1).to_broadcast([128, 16, 16]),
                                                              in1=ab16[:, p, :].unsqueeze(2).to_broadcast([128, 16, 16]), op=ALU.is_equal),
                             reads=[tkb, ib], writes=[tkb])
                        S.op("dve", lambda e: e.tensor_tensor(out=g16[:], in0=g16[:],
                                                              in1=i16[:, p, :].unsqueeze(1).to_broadcast([128, 16, 16]), op=ALU.mult),
                             reads=[tkb], writes=[tkb])
                        S.op("dve", lambda e: e.tensor_reduce(out=e2[:, p, :], in_=g16[:], axis=AX.X, op=ALU.add), reads=[tkb], writes=[tkb])
                    S.op("dve", lambda e: e.tensor_tensor(out=eidf[:, h * 16:(h + 1) * 16], in0=e2[:, 0, :], in1=e2[:, 1, :], op=ALU.add),
                         reads=[tkb], writes=[tkb])
                    S.op("dve", lambda e: e.tensor_single_scalar(out=nmax[:, h:h + 1], in_=c16[:, 0:1], scalar=-1.0, op=ALU.mult),
                         reads=[tkb], writes=[tkb])
                    S.op("act", lambda e: e.activation(out=gate[:, h * 16:(h + 1) * 16], in_=c16[:], func=AF.Exp, bias=nmax[:, h:h + 1],
                                                       scale=1.0, accum_out=gsum[:, h:h + 1]), reads=[tkb], writes=[eb, tkb])
                    yield
                S.op("dve", lambda e: e.reciprocal(out=gsum[:], in_=gsum[:]), reads=[tkb], writes=[tkb])
                S.op("dve", lambda e: e.tensor_tensor(out=gate[:].rearrange("p (h k) -> p h k", h=8),
                                                      in0=gate[:].rearrange("p (h k) -> p h k", h=8),
                                                      in1=gsum[:].unsqueeze(2).to_broadcast([128, 8, 16]), op=ALU.mult),
                     reads=[eb, tkb], writes=[eb])
                S.op("dve", lambda e: e.tensor_copy(out=eidx[:], in_=eidf[:]), reads=[tkb, eb], writes=[eb])
                ctx.update(x=x, xb=xb_, eidx=eidx, gate=gate, eb=eb, xbf=xbf, xbfb=xbfb)
                yield

            dots_r = Ring([sb("pdots%d" % i, [128, 128], F32) for i in range(2)])
            wgt_r = Ring([sb("pwgt%d" % i, [128, 128], F32) for i in range(2)])

            def u_start(c):
                c["dots"], c["db"] = dots_r.next()
                c["wgt"], _ = wgt_r.next()

            def u_slot(c, sl):
                r, rb_ = rows.next()
                S.dma("pool", lambda e: e.indirect_dma_start(out=r[:], out_offset=None, in_=ub_d[:, :],
                                                             in_offset=bass.IndirectOffsetOnAxis(ap=c["eidx"][:, sl:sl + 1], axis=0)),
                      reads=[c["eb"]], writes=[rb_])
                pr, prb = junk_r.next()
                S.op("dve", lambda e: e.tensor_tensor(out=pr[:], in0=r[:], in1=c["xbf"][:], op=ALU.mult), reads=[rb_, c["xbfb"]], writes=[prb])
                S.op("act", lambda e: e.activation(out=pr[:], in_=pr[:], func=AF.Copy, accum_out=c["dots"][:, sl:sl + 1]),
                     reads=[prb], writes=[prb, c["db"]])

            def u_fin(c):
                S.op("act", lambda e: e.activation(out=c["wgt"][:], in_=c["dots"][:], func=AF.Gelu), reads=[c["db"]], writes=[c["db"]])
                S.op("dve", lambda e: e.tensor_tensor(out=c["wgt"][:], in0=c["wgt"][:], in1=c["gate"][:], op=ALU.mult),
                     reads=[c["db"], c["eb"]], writes=[c["db"]])

            def v_slot(c, sl):
                r, rb_ = rows.next()
                S.dma("pool", lambda e: e.indirect_dma_start(out=r[:], out_offset=None, in_=vb_d[:, :],
                                                             in_offset=bass.IndirectOffsetOnAxis(ap=c["eidx"][:, sl:sl + 1], axis=0)),
                      reads=[c["eb"]], writes=[rb_])
                dg, dgb = dgr.next()
                S.op("act", lambda e: e.activation(out=dg[:], in_=self.ident[:], func=AF.Copy, scale=c["wgt"][:, sl:sl + 1]),
                     reads=[c["db"], self.ident_b], writes=[dgb])
                for nb in range(4):
                    S.op("pe", lambda e: e.matmul(self.ps[4 + nb][:, :], lhsT=dg[:], rhs=r[:, nb * 512:(nb + 1) * 512],
                                                  start=sl == 0, stop=sl == 127), reads=[dgb, rb_], writes=[self.psb[4 + nb]])

            def v_fin(c, tt):
                for nb in range(4):
                    S.op("dve", lambda e: e.scalar_tensor_tensor(out=z[:, nb * 512:(nb + 1) * 512], in0=c["x"][:, nb * 512:(nb + 1) * 512],
                                                                 scalar=ALPHA, in1=self.ps[4 + nb][:, :], op0=ALU.mult, op1=ALU.add),
                         reads=[c["xb"], self.psb[4 + nb]], writes=[zb])
                self.ln_store(z, zb, gt, bt, gbb, out_d[tt * 128:(tt + 1) * 128, :], st, stb)

            ctxs = [dict() for _ in range(NT)]
            for _ in gen_a(0, ctxs[0]):
                pass
            for s_ in range(-1, NT):
                tv, tu, ta = s_, s_ + 1, s_ + 2
                gen = gen_a(ta, ctxs[ta]) if ta < NT else None
                if tu < NT:
                    u_start(ctxs[tu])
                for sl0 in range(0, 128, GRP):
                    if tv >= 0:
                        for sl in range(sl0, sl0 + GRP):
                            v_slot(ctxs[tv], sl)
                    if tu < NT:
                        for sl in range(sl0, sl0 + GRP):
                            u_slot(ctxs[tu], sl)
                    if gen is not None:
                        for _ in range(GRP // 8):
                            next(gen, None)
                if gen is not None:
                    for _ in gen:
                        pass
                if tv >= 0:
                    v_fin(ctxs[tv], tv)
                if tu < NT:
                    u_fin(ctxs[tu])
            S.barrier()


IN_SPECS = [
    ("x", [SEQ, D], F32), ("mem", [NMEM, D], F32), ("pos", [1, SEQ], I32),
    ("attn_w_in", [D, 5120], F32), ("attn_w_rot", [D, 3072], F32), ("attn_lambda", [1, 256], F32),
    ("attn_head_norm", [1, 1536], F32), ("mlstm_w_in", [D, 6704], F32), ("gate_bias", [48, 1], F32),
    ("mlstm_head_norm", [1, 1536], F32), ("mem_w_kv", [D, 1024], F32), ("w_out", [2, D, D], F32),
    ("ln_mix_g", [2, D], F32), ("ln_mix_b", [2, D], F32), ("peer_w_query", [2, D, D], F32),
    ("peer_keysT", [2, 128, 16, 128], F32), ("peer_u0", [16384, D], F32), ("peer_u1", [16384, D], F32), ("peer_v0", [16384, D], F32), ("peer_v1", [16384, D], F32),
    ("ln_ffn_g", [2, D], F32), ("ln_ffn_b", [2, D], F32),
    ("c_ident", [128, 128], F32), ("c_rope", [128, 2], F32), ("c_dsel", [48, 3, 24], F32),
    ("c_selr", [48, 24, 128], F32), ("c_tri", [128, 2, 128], F32), ("c_iota", [128, 256], F32),
]


def build_nc(phases=("mem", "attn", "lin0", "peer0", "mlstm", "lin1", "peer1"), debug_out=None):
    nc = bass.Bass("TRN2", target_bir_lowering=False)
    T = {}
    for name, shape, dt in IN_SPECS:
        T[name] = nc.dram_tensor(name, shape, dt, kind="ExternalInput").ap()
    out = nc.dram_tensor("out", [SEQ, D], F32, kind="ExternalOutput").ap()
    h1 = nc.dram_tensor("h1", [SEQ, D], F32, kind="Internal").ap()
    h2 = nc.dram_tensor("h2", [SEQ, D], F32, kind="Internal").ap()
    h3 = nc.dram_tensor("h3", [SEQ, D], F32, kind="Internal").ap()
    mixk = "ExternalOutput" if debug_out == "mix" else "Internal"
    mix_d = nc.dram_tensor("mix_d", [SEQ, D], BF16, kind=mixk).ap()
    tb = {}
    for nm in ("u0", "v0", "u1", "v1"):
        tb[nm] = nc.dram_tensor("tb_" + nm, [16384, D], BF16, kind="Internal").ap()
    with ExitStack() as es:
        S = Sched(nc, es)
        B = Builder(nc, es, S, T)
        ph = list(phases)
        cv = lambda li: [(T["peer_%s%d" % (k, li)], tb["%s%d" % (k, li)]) for k in ("u", "v")]
        conv = []
        if "peer0" in ph and "attn" not in ph:
            conv += cv(0)
        if "peer1" in ph and "mlstm" not in ph:
            conv += cv(1)
        if conv:
            B.phase_convert(conv)
        last = {"lin0": h1, "peer0": h2, "lin1": h3, "peer1": out}
        dst = lambda p: out if p == ph[-1] else last[p]
        if "mem" in ph:
            B.phase_mem()
        if "attn" in ph:
            B.phase_attn(T["x"], mix_d, cv(0) if "peer0" in ph else None)
        if "lin0" in ph:
            B.phase_linear(mix_d, T["w_out"][0], T["x"], T["ln_mix_g"][0:1, :], T["ln_mix_b"][0:1, :], dst("lin0"))
        if "peer0" in ph:
            B.phase_peer(0, h1 if "lin0" in ph else T["x"], dst("peer0"), tb["u0"], tb["v0"])
        if "mlstm" in ph:
            B.phase_mlstm(h2 if "peer0" in ph else T["x"], mix_d, cv(1) if "peer1" in ph else None)
        if "lin1" in ph:
            B.phase_linear(mix_d, T["w_out"][1], h2 if "peer0" in ph else T["x"], T["ln_mix_g"][1:2, :], T["ln_mix_b"][1:2, :], dst("lin1"))
        if "peer1" in ph:
            B.phase_peer(1, h3 if "lin1" in ph else T["x"], dst("peer1"), tb["u1"], tb["v1"])
        S.finish()
        print("instructions:", S.nops)
    return nc


def host_consts():
    c = {}
    c["c_ident"] = np.eye(128, dtype=np.float32)
    p = np.arange(128)
    f = (p % 32).astype(np.float64)
    inv = np.power(10000.0, -f * (2.0 / 64.0))
    rope = np.zeros((128, 2), np.float32)
    rope[:, 0] = (inv.astype(np.float32).astype(np.float64) / (2.0 * np.pi)).astype(np.float32)
    rope[:, 1] = np.where((p % 64) < 32, -1.0, 1.0)
    c["c_rope"] = rope
    dsel = np.zeros((48, 3, 24), np.float32)
    for n in range(12):
        dsel[n, 0, n] = 1.0
        dsel[24 + n, 0, 12 + n] = 1.0
        dsel[12 + n, 1, n] = -1.0
        dsel[36 + n, 2, 12 + n] = -1.0
    c["c_dsel"] = dsel
    selr = np.zeros((48, 24, 128), np.float32)
    for n in range(12):
        selr[12 + n, n, :] = 1.0
        selr[36 + n, 12 + n, :] = 1.0
    c["c_selr"] = selr
    s = np.arange(128)[:, None]
    j = np.arange(128)[None, :]
    tri = np.zeros((128, 2, 128), np.float32)
    tri[:, 0, :] = (s <= j)
    tri[:, 1, :] = (s >= j)
    c["c_tri"] = tri
    c["c_iota"] = np.tile(np.arange(256, dtype=np.float32)[None, :], (128, 1))
    return c


def host_shared(inp):
    sh = dict(host_consts())
    w = np.ascontiguousarray(inp["attn_w_in"][0])
    sh["attn_w_in"] = w
    cols = np.arange(3072)
    r = cols % 64
    partner = np.where(r < 32, cols + 32, cols - 32)
    sh["attn_w_rot"] = np.ascontiguousarray(w[:, partner])
    sh["attn_lambda"] = np.ascontiguousarray(inp["attn_lambda"][0].reshape(1, 256))
    sh["attn_head_norm"] = np.ascontiguousarray(inp["attn_head_norm"][0].reshape(1, 1536))
    sh["mlstm_w_in"] = np.ascontiguousarray(inp["mlstm_w_in"][0])
    sh["gate_bias"] = np.ascontiguousarray(inp["mlstm_gate_bias"][0].reshape(48, 1))
    sh["mlstm_head_norm"] = np.ascontiguousarray(inp["mlstm_head_norm"][0].reshape(1, 1536))
    for k in ("mem_w_kv", "w_out", "ln_mix_g", "ln_mix_b", "peer_w_query", "ln_ffn_g", "ln_ffn_b"):
        sh[k] = np.ascontiguousarray(inp[k])
    for li in range(2):
        sh["peer_u%d" % li] = np.ascontiguousarray(inp["peer_u"][li])
        sh["peer_v%d" % li] = np.ascontiguousarray(inp["peer_v"][li])
    sk = inp["peer_sub_keys"].reshape(2, 16, 128, 128)
    sh["peer_keysT"] = np.ascontiguousarray(sk.transpose(0, 3, 1, 2))
    return {k: np.asarray(v, dtype=(np.int32 if k == "pos" else np.float32)) for k, v in sh.items()}


def kernel(**inputs):
    inp = {k: np.asarray(v) for k, v in inputs.items()}
    shared = host_shared(inp)
    nc = build_nc()
    in_maps = []
    for b in range(8):
        m = dict(shared)
        m["x"] = np.ascontiguousarray(inp["x"][b], dtype=np.float32)
        m["mem"] = np.ascontiguousarray(inp["mem"][b], dtype=np.float32)
        m["pos"] = np.ascontiguousarray(inp["positions"][b].reshape(1, SEQ), dtype=np.int32)
        in_maps.append(m)
    res = run_bass_kernel_spmd(nc, in_maps, core_ids=list(range(8)))
    return np.stack([np.asarray(r["out"], dtype=np.float32) for r in res.results], axis=0)
```
